# Optimizing a Trainium2 kernel written in Bass

```python
import math
import jax, jax.numpy as jnp
from jax import lax
import numpy as np

D_MODEL = 2048
BATCH = 4
SEQ = 2048
DEPTH = 4

CHUNK = 64
N_MIXERS = 2
EPS = 1e-6

GLA_HEADS = 4
GLA_KEY_WIDTH = D_MODEL // 2
GLA_VAL_WIDTH = D_MODEL
GLA_DK = GLA_KEY_WIDTH // GLA_HEADS
GLA_DV = GLA_VAL_WIDTH // GLA_HEADS
GLA_GATE_RANK = 16
GLA_GATE_TEMP = 16.0
GLA_IN_WIDTH = 2 * GLA_KEY_WIDTH + 2 * GLA_VAL_WIDTH + GLA_GATE_RANK

S5_WIDTH = D_MODEL // 2
S5_GROUP = 16
S5_GROUPS = S5_WIDTH // S5_GROUP
S5_STATE = 64
S5_DT_MIN = 1e-3
S5_DT_MAX = 1e-1
S5_EIG_CLIP = -1e-4

MLP_HIDDEN = 4 * D_MODEL

kernel_name = "hybrid_gla_s5_stream_block"


def _rmsnorm(x, g):
    xf = x.astype(jnp.float32)
    y = xf * lax.rsqrt(jnp.mean(xf * xf, axis=-1, keepdims=True) + EPS)
    return (y * g.astype(jnp.float32)).astype(x.dtype)


def _gla_mixer(h, w_in, w_gate_up, b_gate, o_norm, w_out):
    bsz, seq, _ = h.shape
    nc = seq // CHUNK
    proj = h @ w_in
    q, k, v, r, g_low = jnp.split(
        proj,
        [GLA_KEY_WIDTH, 2 * GLA_KEY_WIDTH, 2 * GLA_KEY_WIDTH + GLA_VAL_WIDTH,
         2 * GLA_KEY_WIDTH + 2 * GLA_VAL_WIDTH], axis=-1)
    log_a = jax.nn.log_sigmoid((g_low @ w_gate_up + b_gate).astype(jnp.float32)) / GLA_GATE_TEMP

    def to_chunks(t, dh):
        return t.reshape(bsz, nc, CHUNK, GLA_HEADS, dh).transpose(1, 0, 3, 2, 4).astype(jnp.float32)

    qc = to_chunks(q, GLA_DK) * (GLA_DK ** -0.5)
    kc = to_chunks(k, GLA_DK)
    vc = to_chunks(v, GLA_DV)
    ac = to_chunks(log_a, GLA_DK)
    cum = jnp.cumsum(ac, axis=3)
    total = cum[:, :, :, -1:, :]
    k_dec = kc * jnp.exp(total - cum)
    chunk_decay = jnp.exp(total[:, :, :, 0, :])

    def step(state, xs):
        q_c, k_c, v_c, d_c = xs
        state = d_c[..., None] * state + jnp.einsum('bhck,bhcv->bhkv', k_c, v_c)
        return state, jnp.einsum('bhck,bhkv->bhcv', q_c, state)

    s0 = jnp.zeros((bsz, GLA_HEADS, GLA_DK, GLA_DV), jnp.float32)
    _, o = lax.scan(step, s0, (qc, k_dec, vc, chunk_decay))
    o = o * lax.rsqrt(jnp.mean(o * o, axis=-1, keepdims=True) + EPS) * o_norm.astype(jnp.float32)
    o = o.transpose(1, 0, 3, 2, 4).reshape(bsz, seq, GLA_VAL_WIDTH).astype(h.dtype)
    return (o * jax.nn.silu(r)) @ w_out


def _s5_mixer(h, w_in, lam_re, lam_im, log_dt, b_re, b_im, c_re, c_im, d_skip, w_out):
    bsz, seq, _ = h.shape
    u = (h @ w_in).astype(jnp.float32)
    ug = u.reshape(bsz, seq, S5_GROUPS, S5_GROUP)
    lr = jnp.minimum(lam_re.astype(jnp.float32), S5_EIG_CLIP)
    li = lam_im.astype(jnp.float32)
    dt = jnp.exp(log_dt.astype(jnp.float32))[:, None]
    mag = jnp.exp(lr * dt)
    ang = li * dt
    ab_re = mag * jnp.cos(ang)
    ab_im = mag * jnp.sin(ang)
    den = lr * lr + li * li
    nr = ab_re - 1.0
    f_re = (nr * lr + ab_im * li) / den
    f_im = (ab_im * lr - nr * li) / den
    br = b_re.astype(jnp.float32)
    bi = b_im.astype(jnp.float32)
    bb_re = f_re[..., None] * br - f_im[..., None] * bi
    bb_im = f_re[..., None] * bi + f_im[..., None] * br
    bu_re = jnp.einsum('gnc,blgc->blgn', bb_re, ug)
    bu_im = jnp.einsum('gnc,blgc->blgn', bb_im, ug)
    a_re = jnp.broadcast_to(ab_re, bu_re.shape)
    a_im = jnp.broadcast_to(ab_im, bu_im.shape)

    def combine(e1, e2):
        a1r, a1i, b1r, b1i = e1
        a2r, a2i, b2r, b2i = e2
        return (a2r * a1r - a2i * a1i,
                a2r * a1i + a2i * a1r,
                a2r * b1r - a2i * b1i + b2r,
                a2r * b1i + a2i * b1r + b2i)

    _, _, x_re, x_im = lax.associative_scan(combine, (a_re, a_im, bu_re, bu_im), axis=1)
    y = (jnp.einsum('gcn,blgn->blgc', c_re.astype(jnp.float32), x_re)
         - jnp.einsum('gcn,blgn->blgc', c_im.astype(jnp.float32), x_im))
    y = y.reshape(bsz, seq, S5_WIDTH) + d_skip.astype(jnp.float32) * u
    y = jax.nn.gelu(y).astype(h.dtype)
    val, gate = jnp.split(y @ w_out, 2, axis=-1)
    return val * jax.nn.sigmoid(gate)


def _sq_relu_mlp(h, w_up, w_down):
    a = jax.nn.relu(h @ w_up)
    return (a * a) @ w_down


def setup_inputs(seed: int = 0) -> dict:
    key = jax.random.key(seed)
    ks = jax.random.split(key, 24)
    n_gla = len(range(0, DEPTH, N_MIXERS))
    n_s5 = len(range(1, DEPTH, N_MIXERS))
    res_scale = (2 * DEPTH) ** -0.5

    def nrm(k, shape, scale):
        return jax.random.normal(k, shape, jnp.float32) * scale

    def gain(k, shape):
        return 1.0 + 0.02 * jax.random.normal(k, shape, jnp.float32)

    lam_im0 = math.pi * jnp.arange(S5_STATE, dtype=jnp.float32)
    return {
        'x': jax.random.normal(ks[0], (BATCH, SEQ, D_MODEL), jnp.float32),
        'gla_norm': gain(ks[1], (n_gla, D_MODEL)),
        'gla_w_in': nrm(ks[2], (n_gla, D_MODEL, GLA_IN_WIDTH), D_MODEL ** -0.5),
        'gla_w_gate_up': nrm(ks[3], (n_gla, GLA_GATE_RANK, GLA_KEY_WIDTH), GLA_GATE_RANK ** -0.5),
        'gla_b_gate': 1.0 + 0.1 * jax.random.normal(ks[4], (n_gla, GLA_KEY_WIDTH), jnp.float32),
        'gla_o_norm': gain(ks[5], (n_gla, GLA_DV)),
        'gla_w_out': nrm(ks[6], (n_gla, GLA_VAL_WIDTH, D_MODEL), GLA_VAL_WIDTH ** -0.5 * res_scale),
        's5_norm': gain(ks[7], (n_s5, D_MODEL)),
        's5_w_in': nrm(ks[8], (n_s5, D_MODEL, S5_WIDTH), D_MODEL ** -0.5),
        's5_lam_re': -0.5 + 0.01 * jax.random.normal(ks[9], (n_s5, S5_GROUPS, S5_STATE), jnp.float32),
        's5_lam_im': lam_im0 + 0.01 * jax.random.normal(ks[10], (n_s5, S5_GROUPS, S5_STATE), jnp.float32),
        's5_log_dt': jax.random.uniform(ks[11], (n_s5, S5_GROUPS), jnp.float32,
                                        minval=math.log(S5_DT_MIN), maxval=math.log(S5_DT_MAX)),
        's5_b_re': nrm(ks[12], (n_s5, S5_GROUPS, S5_STATE, S5_GROUP), (2 * S5_GROUP) ** -0.5),
        's5_b_im': nrm(ks[13], (n_s5, S5_GROUPS, S5_STATE, S5_GROUP), (2 * S5_GROUP) ** -0.5),
        's5_c_re': nrm(ks[14], (n_s5, S5_GROUPS, S5_GROUP, S5_STATE), (2 * S5_STATE) ** -0.5),
        's5_c_im': nrm(ks[15], (n_s5, S5_GROUPS, S5_GROUP, S5_STATE), (2 * S5_STATE) ** -0.5),
        's5_d': jax.random.normal(ks[16], (n_s5, S5_WIDTH), jnp.float32),
        's5_w_out': nrm(ks[17], (n_s5, S5_WIDTH, 2 * D_MODEL), S5_WIDTH ** -0.5 * res_scale),
        'mlp_norm': gain(ks[18], (DEPTH, D_MODEL)),
        'mlp_w_up': nrm(ks[19], (DEPTH, D_MODEL, MLP_HIDDEN), D_MODEL ** -0.5),
        'mlp_w_down': nrm(ks[20], (DEPTH, MLP_HIDDEN, D_MODEL), MLP_HIDDEN ** -0.5 * res_scale),
        'final_norm': gain(ks[21], (D_MODEL,)),
    }


def reference(x, gla_norm, gla_w_in, gla_w_gate_up, gla_b_gate, gla_o_norm, gla_w_out,
              s5_norm, s5_w_in, s5_lam_re, s5_lam_im, s5_log_dt, s5_b_re, s5_b_im,
              s5_c_re, s5_c_im, s5_d, s5_w_out, mlp_norm, mlp_w_up, mlp_w_down, final_norm):
    h = x
    for i in range(DEPTH):
        j = i // N_MIXERS
        if i % N_MIXERS == 0:
            h = h + _gla_mixer(_rmsnorm(h, gla_norm[j]), gla_w_in[j], gla_w_gate_up[j],
                               gla_b_gate[j], gla_o_norm[j], gla_w_out[j])
        else:
            h = h + _s5_mixer(_rmsnorm(h, s5_norm[j]), s5_w_in[j], s5_lam_re[j], s5_lam_im[j],
                              s5_log_dt[j], s5_b_re[j], s5_b_im[j], s5_c_re[j], s5_c_im[j],
                              s5_d[j], s5_w_out[j])
        h = h + _sq_relu_mlp(_rmsnorm(h, mlp_norm[i]), mlp_w_up[i], mlp_w_down[i])
    return _rmsnorm(h, final_norm)
```

```python
import math
import contextlib
import numpy as np
import concourse.bass as bass
import concourse.mybir as mybir
from concourse.bass_utils import run_bass_kernel_spmd

F32 = mybir.dt.float32
BF16 = mybir.dt.bfloat16
ALU = mybir.AluOpType
AF = mybir.ActivationFunctionType

D = 2048
NDT = 16
SEQ = 2048
BATCH = 4
DEPTH = 4
EPS = 1e-6
HID = 8192
GLA_IN = 6160
DK = 256
DV = 512

ENGS = ["pe", "act", "dve", "pool", "sp"]
NDMA = 24


class Op:
    __slots__ = ("eng", "fn", "deps", "needed", "count", "is_dma", "slot",
                 "val", "wait_only", "is_cc")

    def __init__(self, eng, fn, is_dma=False):
        self.eng = eng
        self.fn = fn
        self.deps = []
        self.needed = False
        self.count = 0
        self.is_dma = is_dma
        self.slot = -1
        self.val = 0
        self.wait_only = False
        self.is_cc = False


class Sched:
    def __init__(self):
        self.ops = {e: [] for e in ENGS}
        self.last_w = {}
        self.readers = {}
        self.dma_rr = 0
        self.dma_last = [None] * NDMA
        self.dma_cnt = [0] * NDMA
        self.cc_last = None
        self.cc_cnt = 0

    def add(self, eng, fn, reads=(), writes=(), dma=False, cc=False):
        op = Op(eng, fn, dma or cc)
        op.is_cc = cc
        if eng != "pe":
            extra = [k for k in reads if isinstance(k, tuple) and k[0] == "ps" and k not in writes]
            if extra:
                writes = list(writes) + extra
        deps = {}
        for k in reads:
            w = self.last_w.get(k)
            if w is not None:
                deps[w] = deps.get(w, 0) | 1
        for k in writes:
            w = self.last_w.get(k)
            if w is not None:
                deps[w] = deps.get(w, 0) | 2
            for r in self.readers.get(k, ()):
                deps[r] = deps.get(r, 0) | 4
        if cc:
            if self.cc_last is not None:
                deps[self.cc_last] = deps.get(self.cc_last, 0) | 2
            self.cc_last = op
            self.cc_cnt += 1
            op.val = self.cc_cnt
        if dma:
            slot = self.dma_rr % NDMA
            self.dma_rr += 1
            prev = self.dma_last[slot]
            if prev is not None:
                deps[prev] = deps.get(prev, 0) | 2
            self.dma_last[slot] = op
            self.dma_cnt[slot] += 1
            op.slot = slot
            op.val = 16 * self.dma_cnt[slot]
        for d, kind in deps.items():
            if d is op:
                continue
            if (not d.is_dma) and d.eng == eng and not (dma or cc):
                if eng == "pe":
                    continue
                if not (kind & 3):
                    continue
            op.deps.append(d)
            d.needed = True
        for k in reads:
            self.readers.setdefault(k, []).append(op)
        for k in writes:
            self.last_w[k] = op
            self.readers[k] = []
        self.ops[eng].append(op)
        return op

    def barrier(self):
        lasts = []
        for e in ENGS:
            for op in reversed(self.ops[e]):
                if not op.is_dma and not op.wait_only:
                    lasts.append(op)
                    break
        dmas = [d for d in self.dma_last if d is not None]
        if self.cc_last is not None:
            dmas.append(self.cc_last)
        for e in ENGS:
            w = Op(e, None)
            w.wait_only = True
            for d in lasts + dmas:
                if d.eng == e and not d.is_dma:
                    continue
                w.deps.append(d)
                d.needed = True
            self.ops[e].append(w)
        self.last_w = {}
        self.readers = {}

    def emit(self, nc, engsems, dmasems, ccsem=None):
        for e in ENGS:
            c = 0
            for op in self.ops[e]:
                if op.is_dma or op.wait_only:
                    continue
                if op.needed:
                    c += 1
                    op.count = c
        sched = self

        def event(d):
            if d.is_cc:
                return ccsem, d.val
            if d.is_dma:
                return dmasems[d.slot], d.val
            return engsems[d.eng], d.count

        def run(e, eng):
            known = {}
            for op in sched.ops[e]:
                for d in op.deps:
                    sem, val = event(d)
                    if known.get(id(sem), 0) < val:
                        eng.wait_ge(sem, val)
                        known[id(sem)] = val
                if op.wait_only:
                    continue
                ins = op.fn(eng)
                if op.is_cc:
                    ins.then_inc(ccsem, 1)
                elif op.is_dma:
                    ins.then_inc(dmasems[op.slot], 16)
                elif op.needed:
                    ins.then_inc(engsems[e], 1)

        with nc.Block() as block:
            @block.tensor
            def _(eng):
                run("pe", eng)

            @block.scalar
            def _(eng):
                run("act", eng)

            @block.vector
            def _(eng):
                run("dve", eng)

            @block.gpsimd
            def _(eng):
                run("pool", eng)

            @block.sync
            def _(eng):
                run("sp", eng)


def MM(out, lhsT, rhs, start, stop):
    return lambda e: e.matmul(out, lhsT=lhsT, rhs=rhs, start=start, stop=stop)


def TR(out, in_, ident):
    return lambda e: e.transpose(out, in_, ident)


def ACT(out, in_, func, bias=None, scale=None):
    kw = {}
    if bias is not None:
        kw["bias"] = bias
    if scale is not None:
        kw["scale"] = scale
    return lambda e: e.activation(out=out, in_=in_, func=func, **kw)


def TT(out, in0, in1, op):
    return lambda e: e.tensor_tensor(out=out, in0=in0, in1=in1, op=op)


def STT(out, in0, scalar, in1, op0, op1):
    return lambda e: e.scalar_tensor_tensor(out=out, in0=in0, scalar=scalar,
                                            in1=in1, op0=op0, op1=op1)


def TS(out, in0, s1, s2, op0, op1=None):
    if op1 is None:
        return lambda e: e.tensor_scalar(out=out, in0=in0, scalar1=s1,
                                         scalar2=s2, op0=op0)
    return lambda e: e.tensor_scalar(out=out, in0=in0, scalar1=s1, scalar2=s2,
                                     op0=op0, op1=op1)


def CP(out, in_):
    return lambda e: e.tensor_copy(out=out, in_=in_)


def RCP(out, in_):
    return lambda e: e.reciprocal(out=out, in_=in_)


def SQRT(out, in_):
    return lambda e: e.sqrt(out=out, in_=in_)


def MSET(ap, v):
    return lambda e: e.memset(ap, v)


def DMA(out, in_):
    return lambda e: e.dma_start(out=out, in_=in_)


class Arena:
    def __init__(self, nc, lo=16512, hi=229376):
        self.nc = nc
        self.lo = lo
        self.hi = hi
        self.p = lo
        self.n = 0

    def alloc(self, shape, dtype, name="t"):
        esz = 2 if dtype == BF16 else 4
        size = esz
        for s in shape[1:]:
            size *= s
        size = (size + 63) // 64 * 64
        off = self.p
        self.p += size
        assert self.p <= self.hi, f"SBUF overflow {name} {self.p}"
        self.n += 1
        return self.nc.alloc_sbuf_tensor_at(f"{name}_{self.n}", list(shape),
                                            dtype, offset=off)

    def mark(self):
        return self.p

    def reset(self, m):
        self.p = m


class Builder:
    def __init__(self, Lc, phases, final_norm=True, hand=None):
        self.Lc = Lc
        self.nc = nc = bass.Bass("TRN2", target_bir_lowering=False)
        self.S = Sched()
        self.A = Arena(nc)
        dt = nc.dram_tensor
        self.xT = dt("xT", [128, NDT, Lc], F32, kind="ExternalInput").ap()
        self.outT = dt("outT", [128, NDT, Lc], F32, kind="ExternalOutput").ap()
        self.hres = dt("hres", [128, NDT, Lc], F32).ap()
        self.gains_d = dt("gains", [128, 9, NDT], F32, kind="ExternalInput").ap()
        self.consts_d = dt("consts", [128, 5, 128], F32, kind="ExternalInput").ap()
        self.gla_w_in = dt("gla_w_in", [2, D, GLA_IN], F32, kind="ExternalInput").ap()
        self.gla_wg = dt("gla_wg", [2, 17, 1024], F32, kind="ExternalInput").ap()
        self.gla_on = dt("gla_on", [2, 128, 4], F32, kind="ExternalInput").ap()
        self.gla_w_out = dt("gla_w_out", [2, D, D], F32, kind="ExternalInput").ap()
        self.s5_w_in = dt("s5_w_in", [2, D, 1024], F32, kind="ExternalInput").ap()
        self.s5_w_out = dt("s5_w_out", [2, 1024, 2 * D], F32, kind="ExternalInput").ap()
        self.s5_lam = dt("s5_lam", [2, 64, 3, 64], F32, kind="ExternalInput").ap()
        self.s5_B = dt("s5_B", [2, 64, 2, 64, 16], F32, kind="ExternalInput").ap()
        self.s5_C = dt("s5_C", [2, 64, 2, 64, 16], F32, kind="ExternalInput").ap()
        self.s5_dcol = dt("s5_dcol", [2, 128, 64], F32, kind="ExternalInput").ap()
        self.s5_E = dt("s5_E", [128, 64, 128], F32, kind="ExternalInput").ap()
        self.hand = (Lc < SEQ) if hand is None else hand
        self.flag_d = dt("flag", [128, 1], F32, kind="ExternalInput").ap()
        self.gs_src = dt("gs_src", [128, 4096], F32).ap()
        self.gs_dst = dt("gs_dst", [256, 4096], F32).ap()
        self.xs_src = dt("xs_src", [64, 128], F32).ap()
        self.xs_dst = dt("xs_dst", [128, 128], F32).ap()
        self.mlp_w_up = dt("mlp_w_up", [DEPTH, D, HID], F32, kind="ExternalInput").ap()
        self.mlp_w_down = dt("mlp_w_down", [DEPTH, HID, D], F32, kind="ExternalInput").ap()

        self.ps = [nc.alloc_psum_tensor(f"ps{i}", [128, 512], F32) for i in range(8)]
        A = self.A
        self.gains = A.alloc([128, 9, NDT], F32, "gains")
        self.consts = A.alloc([128, 5, 128], F32, "consts")
        self.sq = [A.alloc([128, 512], F32, "sq") for _ in range(2)]
        self.rstd = A.alloc([128, 512], F32, "rstd")
        self.flag = A.alloc([128, 1], F32, "flag")
        self.persist_mark = A.mark()
        S = self.S
        S.add("sp", DMA(self.gains[:], self.gains_d), writes=["gains"], dma=True)
        S.add("sp", DMA(self.consts[:], self.consts_d), writes=["consts"], dma=True)
        S.add("sp", DMA(self.flag[:], self.flag_d), writes=["flag"], dma=True)
        self.ident = self.consts[:, 0, :]
        self.tri2 = self.consts[:, 1, :]
        self.ones = self.consts[:, 2, :]
        self.s5mask = self.consts[:, 3, :]
        self.ind = self.consts[:, 4, 0:2]
        self.first_res = True
        self.hs_n = 0

        for ph in phases:
            kind = ph[0]
            if kind == "gla":
                self.gla_phase(ph[1])
            elif kind == "s5":
                self.s5_phase(ph[1])
            elif kind == "mlp":
                self.mlp_phase(ph[1])
        self.final_phase(final_norm)
        S.barrier()
        with contextlib.ExitStack() as st:
            engsems = {e: st.enter_context(nc.semaphore(f"s_{e}")) for e in ENGS}
            dmasems = [st.enter_context(nc.semaphore(f"d_{i}")) for i in range(NDMA)]
            ccsem = st.enter_context(nc.semaphore("ccsem"))
            S.emit(nc, engsems, dmasems, ccsem)

    def res_src(self):
        return self.xT if self.first_res else self.hres

    def phase_begin(self):
        self.S.barrier()
        self.A.reset(self.persist_mark)

    def load_h(self, hb, t0, ntok, key):
        src = self.res_src()
        ntt = ntok // 512
        for q in range(4):
            self.S.add("sp", DMA(hb[:, 4 * q:4 * q + 4, :], src[:, 4 * q:4 * q + 4, t0:t0 + ntok]),
                       writes=[(key, d_, tt) for d_ in range(4 * q, 4 * q + 4) for tt in range(ntt)],
                       dma=True)

    def rmsnorm(self, hb, hn, ntok, grow, hkey, nkey):
        S = self.S
        for tt in range(ntok // 512):
            sl = slice(tt * 512, (tt + 1) * 512)
            ps = self.ps[7]
            for d_ in range(NDT):
                sq = self.sq[d_ % 2]
                S.add("act", ACT(sq[:], hb[:, d_, sl], AF.Square),
                      reads=[(hkey, d_, tt)], writes=[("sq", d_ % 2)])
                S.add("pe", MM(ps[:], self.ones, sq[:], d_ == 0, d_ == NDT - 1),
                      reads=[("sq", d_ % 2), "consts"], writes=[("ps", 7)])
            S.add("dve", TS(self.rstd[:], ps[:], 1.0 / D, EPS, ALU.mult, ALU.add),
                  reads=[("ps", 7)], writes=["rstd"])
            S.add("act", SQRT(self.rstd[:], self.rstd[:]), reads=["rstd"], writes=["rstd"])
            S.add("dve", RCP(self.rstd[:], self.rstd[:]), reads=["rstd"], writes=["rstd"])
            for d_ in range(NDT):
                eng = "dve"
                S.add(eng, STT(hn[:, d_, sl], hb[:, d_, sl], self.gains[:, grow, d_:d_ + 1],
                               self.rstd[:], ALU.mult, ALU.mult),
                      reads=[(hkey, d_, tt), "rstd", "gains"], writes=[(nkey, d_, tt)])

    def rmsnorm_stream(self, t0, grow, hn, hs, nkey="hn"):
        S = self.S
        src = self.res_src()
        ps = self.ps[7]
        for q in range(4):
            b = self.hs_n % 2
            self.hs_n += 1
            S.add("sp", DMA(hs[b][:], src[:, 4 * q:4 * q + 4, t0:t0 + 512]), writes=[("hs", b)], dma=True)
            for r in range(4):
                d_ = 4 * q + r
                sq = self.sq[d_ % 2]
                S.add("act", ACT(sq[:], hs[b][:, r, :], AF.Square), reads=[("hs", b)], writes=[("sq", d_ % 2)])
                S.add("pe", MM(ps[:], self.ones, sq[:], d_ == 0, d_ == NDT - 1),
                      reads=[("sq", d_ % 2), "consts"], writes=[("ps", 7)])
        S.add("dve", TS(self.rstd[:], ps[:], 1.0 / D, EPS, ALU.mult, ALU.add),
              reads=[("ps", 7)], writes=["rstd"])
        S.add("act", SQRT(self.rstd[:], self.rstd[:]), reads=["rstd"], writes=["rstd"])
        S.add("dve", RCP(self.rstd[:], self.rstd[:]), reads=["rstd"], writes=["rstd"])
        for q in range(4):
            b = self.hs_n % 2
            self.hs_n += 1
            S.add("sp", DMA(hs[b][:], src[:, 4 * q:4 * q + 4, t0:t0 + 512]), writes=[("hs", b)], dma=True)
            for r in range(4):
                d_ = 4 * q + r
                eng = "dve"
                S.add(eng, STT(hn[:, d_, :], hs[b][:, r, :], self.gains[:, grow, d_:d_ + 1],
                               self.rstd[:], ALU.mult, ALU.mult),
                      reads=[("hs", b), "rstd", "gains"], writes=[(nkey, d_, 0)])

    def mlp_phase(self, layer):
        S, A = self.S, self.A
        MC = 512
        NMC = HID // MC
        for st in range(self.Lc // 1024):
            self.phase_begin()
            t0 = st * 1024
            hb = A.alloc([128, NDT, 1024], F32, "mlp_h")
            hn = A.alloc([128, NDT, 1024], BF16, "mlp_hn")
            wup = [A.alloc([128, NDT, MC], BF16, "wup") for _ in range(3)]
            wdn = [A.alloc([128, MC // 128, D], BF16, "wdn") for _ in range(2)]
            hid = [A.alloc([128, MC // 128, 1024], BF16, "hid") for _ in range(2)]
            tmp = [A.alloc([128, 512], F32, "rtmp") for _ in range(2)]
            w_up = self.mlp_w_up
            w_dn = self.mlp_w_down
            cnt = {"ev": 0, "dbank": 0}

            def ld_up(mc):
                if mc >= NMC:
                    return
                b3 = mc % 3
                S.add("pool", DMA(wup[b3][:], w_up[layer, :, mc * MC:(mc + 1) * MC]
                                  .rearrange("(kt p) m -> p kt m", p=128)),
                      writes=[("wup", b3)], dma=True)

            def ld_dn(mc):
                if mc >= NMC:
                    return
                b = mc % 2
                S.add("pool", DMA(wdn[b][:], w_dn[layer, mc * MC:(mc + 1) * MC, :]
                                  .rearrange("(kt p) d -> p kt d", p=128)),
                      writes=[("wdn", b)], dma=True)

            def up(mc):
                b = mc % 2
                b3 = mc % 3
                for mt in range(MC // 128):
                    pb = [self.ps[(mt % 2) * 2 + tt] for tt in range(2)]
                    pk = [("ps", (mt % 2) * 2 + tt) for tt in range(2)]
                    for kt in range(NDT):
                        for tt in range(2):
                            S.add("pe", MM(pb[tt][:], wup[b3][:, kt, mt * 128:(mt + 1) * 128],
                                           hn[:, kt, tt * 512:(tt + 1) * 512], kt == 0, kt == NDT - 1),
                                  reads=[("wup", b3), ("hn", kt, tt)], writes=[pk[tt]])
                    for tt in range(2):
                        ev = cnt["ev"]
                        tb = tmp[ev % 2]
                        tk = ("rtmp", ev % 2)
                        cnt["ev"] = ev + 1
                        S.add("act", ACT(tb[:], pb[tt][:], AF.Relu), reads=[pk[tt]], writes=[tk])
                        S.add("act", ACT(hid[b][:, mt, tt * 512:(tt + 1) * 512], tb[:], AF.Square),
                              reads=[tk], writes=[("hid", b, mt, tt)])

            def down(mc):
                b = mc % 2
                for d_ in range(NDT):
                    for tt in range(2):
                        bank = 4 + cnt["dbank"] % 3
                        cnt["dbank"] += 1
                        pbk = self.ps[bank]
                        nk = MC // 128
                        for j in range(nk):
                            S.add("pe", MM(pbk[:], wdn[b][:, j, d_ * 128:(d_ + 1) * 128],
                                           hid[b][:, j, tt * 512:(tt + 1) * 512], j == 0, j == nk - 1),
                                  reads=[("wdn", b), ("hid", b, j, tt)], writes=[("ps", bank)])
                        sl = slice(tt * 512, (tt + 1) * 512)
                        S.add("dve", TT(hb[:, d_, sl], pbk[:], hb[:, d_, sl], ALU.add),
                              reads=[("ps", bank), ("h", d_, tt)], writes=[("h", d_, tt)])

            ld_up(0)
            ld_dn(0)
            ld_up(1)
            self.load_h(hb, t0, 1024, "h")
            self.rmsnorm(hb, hn, 1024, 4 + layer, "h", "hn")
            up(0)
            for mc in range(NMC):
                ld_up(mc + 2)
                ld_dn(mc + 1)
                if mc + 1 < NMC:
                    up(mc + 1)
                down(mc)
            for q in range(4):
                S.add("sp", DMA(self.hres[:, 4 * q:4 * q + 4, t0:t0 + 1024], hb[:, 4 * q:4 * q + 4, :]),
                      reads=[("h", d_, tt) for d_ in range(4 * q, 4 * q + 4) for tt in range(2)],
                      dma=True)
        self.first_res = False

    def gla_phase(self, j):
        S, A = self.S, self.A
        NB = 512
        self.phase_begin()
        gla_S = A.alloc([128, 8, 512], F32, "glaS")
        hs = [A.alloc([128, 4, NB], F32, "g_hs") for _ in range(2)]
        hn = A.alloc([128, NDT, NB], BF16, "g_hn")
        wsl = [A.alloc([128, NDT, 512], BF16, "g_wsl") for _ in range(2)]
        gl = A.alloc([32, NB], F32, "g_gl")
        wg = A.alloc([32, 1024], F32, "g_wg")
        onc = A.alloc([128, 4], F32, "g_on")
        la = A.alloc([128, 1024], F32, "g_la")
        etmp = A.alloc([128, 1024], F32, "g_etmp")
        dec = A.alloc([128, 4, 1024], F32, "g_dec")
        cd = A.alloc([128, 8, 8], F32, "g_cd")
        kdec = A.alloc([128, 4, 1024], BF16, "g_kdec")
        vv = A.alloc([128, 4, 2048], BF16, "g_v")
        rT = vv[:].rearrange("p a (b c) -> p (a b) c", c=NB)
        qT = A.alloc([128, 8, NB], BF16, "g_qT")
        Sb = A.alloc([128, 8, 512], BF16, "g_Sb")
        oT = A.alloc([128, 16, NB], BF16, "g_oT")
        ssb = A.alloc([128, 4, NB], F32, "g_ssb")
        sqo = A.alloc([128, 512], F32, "g_sqo")
        utmp = A.alloc([128, 512], F32, "g_utmp")
        w_in = self.gla_w_in
        w_out = self.gla_w_out
        S.add("sp", DMA(wg[0:17, :], self.gla_wg[j]), writes=["wg"], dma=True)
        S.add("sp", DMA(onc[:], self.gla_on[j]), writes=["onc"], dma=True)
        S.add("pool", MSET(gl[:], 1.0), writes=["gl"])
        S.add("dve", MSET(gla_S[:], 0.0), writes=[("S", i) for i in range(8)])
        nslab = [0]
        SSB = [("ssb", c) for c in range(8)]

        def load_slab(src_ap, ncols):
            b = nslab[0] % 2
            nslab[0] += 1
            S.add("pool", DMA(wsl[b][:, :, 0:ncols], src_ap.rearrange("(kt p) m -> p kt m", p=128)),
                  writes=[("wsl", b)], dma=True)
            return b

        nblk = self.Lc // NB
        PAIRS = [[0, 1], [2, 3], [4, 5], [6, 7]]
        for blk in range(2 * nblk if self.hand else nblk):
            pre = self.hand and blk < nblk
            if self.hand and blk == nblk:
                SK = [("S", i) for i in range(8)]
                S.add("sp", DMA(self.gs_src, gla_S[:].rearrange("p a b -> p (a b)")), reads=SK, writes=["gs_src"], dma=True)
                S.add("pool", (lambda e: e.collective_compute("AllGather", ALU.bypass, replica_groups=PAIRS,
                                                              ins=[self.gs_src], outs=[self.gs_dst])),
                      reads=["gs_src"], writes=["gs_dst"], cc=True)
                S.add("sp", DMA(gla_S[:].rearrange("p a b -> p (a b)"), self.gs_dst[0:128, :]),
                      reads=["gs_dst"], writes=SK, dma=True)
                for i in range(8):
                    S.add("dve", TS(gla_S[:, i, :], gla_S[:, i, :], self.flag[:, 0:1], None, ALU.mult),
                          reads=[("S", i), "flag"], writes=[("S", i)])
            blk = blk % nblk
            t0 = blk * NB
            self.rmsnorm_stream(t0, j, hn, hs)
            b = load_slab(w_in[j, :, 6144:6160], 16)
            for kt in range(NDT):
                S.add("pe", MM(self.ps[0][0:16, :], wsl[b][:, kt, 0:16], hn[:, kt, :], kt == 0, kt == NDT - 1),
                      reads=[("wsl", b), ("hn", kt, 0)], writes=[("ps", 0)])
            S.add("dve", CP(gl[0:16, :], self.ps[0][0:16, :]), reads=[("ps", 0)], writes=["gl"])
            for tt in range(4):
                for hf in range(2):
                    S.add("pe", MM(self.ps[1 + hf][:], gl[0:17, tt * 128:(tt + 1) * 128],
                                   wg[0:17, hf * 512:(hf + 1) * 512], True, True),
                          reads=["gl", "wg"], writes=[("ps", 1 + hf)])
                    S.add("act", ACT(etmp[:, hf * 512:(hf + 1) * 512], self.ps[1 + hf][:], AF.Exp, scale=-1.0),
                          reads=[("ps", 1 + hf)], writes=[("etmp", hf)])
                for hf in range(2):
                    S.add("act", ACT(la[:, hf * 512:(hf + 1) * 512], etmp[:, hf * 512:(hf + 1) * 512],
                                     AF.Ln, bias=1.0), reads=[("etmp", hf)], writes=[("la", hf)])
                for hf in range(2):
                    S.add("pe", MM(self.ps[1 + hf][:], self.tri2, la[:, hf * 512:(hf + 1) * 512], True, True),
                          reads=["consts", ("la", hf)], writes=[("ps", 1 + hf)])
                    S.add("act", ACT(dec[:, tt, hf * 512:(hf + 1) * 512], self.ps[1 + hf][:], AF.Exp),
                          reads=[("ps", 1 + hf)], writes=[("dec", tt, hf)])
                for ft in range(8):
                    S.add("pe", MM(self.ps[3][:, ft * 2:ft * 2 + 2], la[:, ft * 128:(ft + 1) * 128], self.ind, True, True),
                          reads=["consts", ("la", ft // 4)], writes=[("ps", 3)])
                S.add("act", ACT(cd[:, :, tt * 2:tt * 2 + 2],
                                 self.ps[3][:, 0:16].rearrange("p (a b) -> p a b", b=2), AF.Exp),
                      reads=[("ps", 3)], writes=[("cd", tt)])
            for sl_ in range(2):
                b = load_slab(w_in[j, :, 1024 + sl_ * 512:1024 + (sl_ + 1) * 512], 512)
                for tt in range(4):
                    bank = 4 + tt % 2
                    for kt in range(NDT):
                        S.add("pe", MM(self.ps[bank][:], hn[:, kt, tt * 128:(tt + 1) * 128], wsl[b][:, kt, :],
                                       kt == 0, kt == NDT - 1),
                              reads=[("wsl", b), ("hn", kt, 0)], writes=[("ps", bank)])
                    S.add("dve", TT(kdec[:, tt, sl_ * 512:(sl_ + 1) * 512], self.ps[bank][:],
                                    dec[:, tt, sl_ * 512:(sl_ + 1) * 512], ALU.mult),
                          reads=[("ps", bank), ("dec", tt, sl_)], writes=[("kdec", tt, sl_)])
            for sl_ in range(4):
                b = load_slab(w_in[j, :, 2048 + sl_ * 512:2048 + (sl_ + 1) * 512], 512)
                for tt in range(4):
                    bank = 4 + tt % 2
                    for kt in range(NDT):
                        S.add("pe", MM(self.ps[bank][:], hn[:, kt, tt * 128:(tt + 1) * 128], wsl[b][:, kt, :],
                                       kt == 0, kt == NDT - 1),
                              reads=[("wsl", b), ("hn", kt, 0)], writes=[("ps", bank)])
                    S.add("act", ACT(vv[:, tt, sl_ * 512:(sl_ + 1) * 512], self.ps[bank][:], AF.Copy),
                          reads=[("ps", bank)], writes=[("v", tt, sl_)])
            for sl_ in range(0 if pre else 2):
                b = load_slab(w_in[j, :, sl_ * 512:(sl_ + 1) * 512], 512)
                for f4 in range(4):
                    bank = 4 + f4 % 2
                    for kt in range(NDT):
                        S.add("pe", MM(self.ps[bank][:], wsl[b][:, kt, f4 * 128:(f4 + 1) * 128], hn[:, kt, :],
                                       kt == 0, kt == NDT - 1),
                              reads=[("wsl", b), ("hn", kt, 0)], writes=[("ps", bank)])
                    S.add("act", ACT(qT[:, sl_ * 4 + f4, :], self.ps[bank][:], AF.Copy, scale=DK ** -0.5),
                          reads=[("ps", bank)], writes=[("qT", sl_ * 4 + f4)])
            for c in range(8):
                tt, hf = c // 2, c % 2
                rows = slice(hf * 64, hf * 64 + 64)
                for h in range(4):
                    for kt in range(2):
                        i = h * 2 + kt
                        bank = i % 2
                        S.add("pe", MM(self.ps[bank][:], kdec[rows, tt, h * 256 + kt * 128:h * 256 + (kt + 1) * 128],
                                       vv[rows, tt, h * 512:(h + 1) * 512], True, True),
                              reads=[("kdec", tt, h // 2), ("v", tt, h)], writes=[("ps", bank)])
                        S.add("dve", STT(gla_S[:, i, :], gla_S[:, i, :], cd[:, i, c:c + 1],
                                         self.ps[bank][:], ALU.mult, ALU.add),
                              reads=[("S", i), ("cd", tt), ("ps", bank)], writes=[("S", i)])
                        S.add("act", ACT(Sb[:, i, :], gla_S[:, i, :], AF.Copy), reads=[("S", i)], writes=[("Sb", i)])
                if pre:
                    continue
                for h2 in range(2):
                    bank = 2 + h2
                    for hh in range(2):
                        h = h2 * 2 + hh
                        for vt in range(4):
                            col = (hh * 4 + vt) * 64
                            for kt in range(2):
                                i = h * 2 + kt
                                S.add("pe", MM(self.ps[bank][:, col:col + 64], Sb[:, i, vt * 128:(vt + 1) * 128],
                                               qT[:, i, c * 64:(c + 1) * 64], kt == 0, kt == 1),
                                      reads=[("Sb", i), ("qT", i)], writes=[("ps", bank)])
                    S.add("dve", CP(oT[:, h2 * 8:h2 * 8 + 8, c * 64:(c + 1) * 64],
                                    self.ps[bank][:].rearrange("p (a b) -> p a b", b=64)),
                          reads=[("ps", bank)],
                          writes=[("oT", h2, c)] + [("gT", i) for i in range(h2 * 8, h2 * 8 + 8)])
                    S.add("act", ACT(sqo[:], self.ps[bank][:], AF.Square),
                          reads=[("ps", bank), ("oT", h2, c)], writes=["sqo"])
                    for hh in range(2):
                        for vt in range(4):
                            col = (hh * 4 + vt) * 64
                            S.add("pe", MM(self.ps[6][:, (h2 * 2 + hh) * 64:(h2 * 2 + hh) * 64 + 64], self.ones,
                                           sqo[:, col:col + 64], vt == 0, vt == 3),
                                  reads=["consts", "sqo"], writes=[("ps", 6)])
                S.add("dve", CP(ssb[:, :, c * 64:(c + 1) * 64],
                                self.ps[6][:, 0:256].rearrange("p (a b) -> p a b", b=64)),
                      reads=[("ps", 6)], writes=[("ssb", c)])
            if pre:
                continue
            S.add("dve", TS(ssb[:], ssb[:], 1.0 / DV, EPS, ALU.mult, ALU.add), reads=SSB, writes=SSB)
            S.add("act", SQRT(ssb[:], ssb[:]), reads=SSB, writes=SSB)
            S.add("dve", RCP(ssb[:], ssb[:]), reads=SSB, writes=SSB)
            for sl_ in range(4):
                b = load_slab(w_in[j, :, 4096 + sl_ * 512:4096 + (sl_ + 1) * 512], 512)
                for f4 in range(4):
                    bank = 4 + f4 % 2
                    i = sl_ * 4 + f4
                    rk = ("v", i // 4, i % 4)
                    for kt in range(NDT):
                        S.add("pe", MM(self.ps[bank][:], wsl[b][:, kt, f4 * 128:(f4 + 1) * 128], hn[:, kt, :],
                                       kt == 0, kt == NDT - 1),
                              reads=[("wsl", b), ("hn", kt, 0)], writes=[("ps", bank)])
                    S.add("act", ACT(rT[:, i, :], self.ps[bank][:], AF.Silu),
                          reads=[("ps", bank)], writes=[rk])
                    S.add("dve", STT(utmp[:], oT[:, i, :], onc[:, f4:f4 + 1], ssb[:, sl_, :], ALU.mult, ALU.mult),
                          reads=[("oT", i // 8, c) for c in range(8)] + SSB + ["onc"],
                          writes=["utmp"])
                    S.add("dve", TT(oT[:, i, :], utmp[:], rT[:, i, :], ALU.mult),
                          reads=["utmp", rk], writes=[("gT", i)])
            src = self.res_src()
            for sl_ in range(4):
                b = load_slab(w_out[j, :, sl_ * 512:(sl_ + 1) * 512], 512)
                hb_ = self.hs_n % 2
                self.hs_n += 1
                S.add("sp", DMA(hs[hb_][:], src[:, sl_ * 4:sl_ * 4 + 4, t0:t0 + NB]),
                      writes=[("hs", hb_)], dma=True)
                for f4 in range(4):
                    bank = 4 + f4 % 2
                    for kt in range(NDT):
                        S.add("pe", MM(self.ps[bank][:], wsl[b][:, kt, f4 * 128:(f4 + 1) * 128], oT[:, kt, :],
                                       kt == 0, kt == NDT - 1),
                              reads=[("wsl", b), ("gT", kt)], writes=[("ps", bank)])
                    S.add("dve", TT(hs[hb_][:, f4, :], self.ps[bank][:], hs[hb_][:, f4, :], ALU.add),
                          reads=[("ps", bank), ("hs", hb_)], writes=[("hs", hb_)])
                S.add("sp", DMA(self.hres[:, sl_ * 4:sl_ * 4 + 4, t0:t0 + NB], hs[hb_][:]),
                      reads=[("hs", hb_)], dma=True)
        self.first_res = False

    def s5_phase(self, j):
        S, A = self.S, self.A
        NB = 512
        J = NB // 8
        TWO_PI = 2.0 * math.pi
        I32 = mybir.dt.int32
        self.phase_begin()
        E = A.alloc([128, 64, 128], BF16, "s5E")
        M_T = A.alloc([128, 64, 128], BF16, "s5M")
        P_T = A.alloc([128, 64, 128], BF16, "s5P")
        Qb = A.alloc([64, 64, 2, 128], BF16, "s5Q")
        A8 = A.alloc([64, 2, 64], F32, "s5A8")
        X = A.alloc([64, 2, 64], F32, "s5X")
        mark_main = A.mark()
        lam = A.alloc([64, 3, 64], F32, "c_lam")
        Bm = A.alloc([64, 2, 64, 16], F32, "c_B")
        Cm = A.alloc([64, 2, 64, 16], F32, "c_C")
        dcol = A.alloc([128, 64], F32, "c_dcol")
        dtv = A.alloc([64, 64], F32, "c_dt")
        lr = A.alloc([64, 64], F32, "c_lr")
        xx = A.alloc([64, 64], F32, "c_xx")
        ang = A.alloc([64, 64], F32, "c_ang")
        mg = A.alloc([64, 16, 64], F32, "c_mg")
        ths = A.alloc([64, 16, 64], F32, "c_ths")
        thc = A.alloc([64, 16, 64], F32, "c_thc")
        ctmp = A.alloc([64, 16, 64], F32, "c_ctmp")
        ki = A.alloc([64, 16, 64], I32, "c_ki")
        ar = A.alloc([64, 16, 64], F32, "c_ar")
        ai = A.alloc([64, 16, 64], F32, "c_ai")
        er = A.alloc([64, 8, 64], F32, "c_er")
        ei = A.alloc([64, 8, 64], F32, "c_ei")
        e1 = A.alloc([64, 8, 64], F32, "c_e1")
        e2 = A.alloc([64, 8, 64], F32, "c_e2")
        fw = [A.alloc([64, 64], F32, "c_fw") for _ in range(6)]
        PS = A.alloc([64, 8, 2, 128], F32, "c_PS")
        Q2 = A.alloc([64, 8, 2, 128], F32, "c_Q2")
        Qf = A.alloc([64, 8, 2, 128], F32, "c_Qf")
        tt_ = [A.alloc([64, 8, 8, 16], F32, "c_t") for _ in range(4)]
        erR = A.alloc([64, 8, 64], F32, "c_erR")
        eiR = A.alloc([64, 8, 64], F32, "c_eiR")
        mtmp = A.alloc([128, 4, 128], F32, "c_mtmp")

        S.add("pool", DMA(E[:], self.s5_E), writes=["E"], dma=True)
        S.add("sp", DMA(lam[:], self.s5_lam[j]), writes=["lam"], dma=True)
        S.add("sp", DMA(Bm[:], self.s5_B[j]), writes=["Bm"], dma=True)
        S.add("sp", DMA(Cm[:], self.s5_C[j]), writes=["Cm"], dma=True)
        S.add("sp", DMA(dcol[:], self.s5_dcol[j]), writes=["dcol"], dma=True)
        S.add("dve", MSET(X[:], 0.0), writes=["X"])
        S.add("act", ACT(dtv[:], lam[:, 2, :], AF.Exp), reads=["lam"], writes=["dtv"])
        S.add("dve", TS(lr[:], lam[:, 0, :], -1e-4, None, ALU.min), reads=["lam"], writes=["lr"])
        S.add("dve", TT(xx[:], lr[:], dtv[:], ALU.mult), reads=["lr", "dtv"], writes=["xx"])
        S.add("dve", TT(ang[:], lam[:, 1, :], dtv[:], ALU.mult), reads=["lam", "dtv"], writes=["ang"])
        for ti, tau in enumerate(range(-7, 9)):
            S.add("act", ACT(mg[:, ti, :], xx[:], AF.Exp, scale=float(tau)), reads=["xx"], writes=["mg"])
            S.add("dve", TS(ths[:, ti, :], ang[:], float(tau), None, ALU.mult), reads=["ang"], writes=["ths"])
        S.add("dve", TS(thc[:], ths[:], math.pi / 2, None, ALU.add), reads=["ths"], writes=["thc"])
        for th, k in ((ths, "ths"), (thc, "thc")):
            S.add("dve", TS(ctmp[:], th[:], 1.0 / TWO_PI, None, ALU.mult), reads=[k], writes=["ctmp"])
            S.add("dve", CP(ki[:], ctmp[:]), reads=["ctmp"], writes=["ki"])
            S.add("dve", CP(ctmp[:], ki[:]), reads=["ki"], writes=["ctmp"])
            S.add("dve", STT(th[:], ctmp[:], -TWO_PI, th[:], ALU.mult, ALU.add), reads=["ctmp", k], writes=[k])
            S.add("dve", TS(ctmp[:], th[:], math.pi, TWO_PI, ALU.is_gt, ALU.mult), reads=[k], writes=["ctmp"])
            S.add("dve", TT(th[:], th[:], ctmp[:], ALU.subtract), reads=[k, "ctmp"], writes=[k])
            S.add("dve", TS(ctmp[:], th[:], -math.pi, TWO_PI, ALU.is_lt, ALU.mult), reads=[k], writes=["ctmp"])
            S.add("dve", TT(th[:], th[:], ctmp[:], ALU.add), reads=[k, "ctmp"], writes=[k])
            S.add("act", ACT(th[:], th[:], AF.Sin), reads=[k], writes=[k])
        S.add("dve", TT(ar[:], mg[:], thc[:], ALU.mult), reads=["mg", "thc"], writes=["ar"])
        S.add("dve", TT(ai[:], mg[:], ths[:], ALU.mult), reads=["mg", "ths"], writes=["ai"])
        nr, den, t1, t2, f_re, f_im = fw
        li = lam[:, 1, :]
        ni = ai[:, 8, :]
        S.add("dve", TS(nr[:], ar[:, 8, :], -1.0, None, ALU.add), reads=["ar"], writes=["nr"])
        S.add("dve", TT(t1[:], lr[:], lr[:], ALU.mult), reads=["lr"], writes=["t1"])
        S.add("dve", TT(t2[:], li, li, ALU.mult), reads=["lam"], writes=["t2"])
        S.add("dve", TT(den[:], t1[:], t2[:], ALU.add), reads=["t1", "t2"], writes=["den"])
        S.add("dve", RCP(den[:], den[:]), reads=["den"], writes=["den"])
        S.add("dve", TT(t1[:], nr[:], lr[:], ALU.mult), reads=["nr", "lr", "den"], writes=["t1"])
        S.add("dve", TT(t2[:], ni, li, ALU.mult), reads=["ai", "lam", "den"], writes=["t2"])
        S.add("dve", TT(t1[:], t1[:], t2[:], ALU.add), reads=["t1", "t2"], writes=["t1"])
        S.add("dve", TT(f_re[:], t1[:], den[:], ALU.mult), reads=["t1", "den"], writes=["f_re"])
        S.add("dve", TT(t1[:], ni, lr[:], ALU.mult), reads=["ai", "lr", "f_re"], writes=["t1"])
        S.add("dve", TT(t2[:], nr[:], li, ALU.mult), reads=["nr", "lam", "f_re"], writes=["t2"])
        S.add("dve", TT(t1[:], t1[:], t2[:], ALU.subtract), reads=["t1", "t2"], writes=["t1"])
        S.add("dve", TT(f_im[:], t1[:], den[:], ALU.mult), reads=["t1", "den"], writes=["f_im"])
        frb = f_re[:].unsqueeze(1).to_broadcast([64, 8, 64])
        fib = f_im[:].unsqueeze(1).to_broadcast([64, 8, 64])
        S.add("dve", TT(e1[:], ar[:, 7:15, :], frb, ALU.mult), reads=["ar", "f_re"], writes=["e1"])
        S.add("dve", TT(e2[:], ai[:, 7:15, :], fib, ALU.mult), reads=["ai", "f_im"], writes=["e2"])
        S.add("dve", TT(er[:], e1[:], e2[:], ALU.subtract), reads=["e1", "e2"], writes=["er"])
        S.add("dve", TT(e1[:], ar[:, 7:15, :], fib, ALU.mult), reads=["ar", "f_im", "er"], writes=["e1"])
        S.add("dve", TT(e2[:], ai[:, 7:15, :], frb, ALU.mult), reads=["ai", "f_re", "er"], writes=["e2"])
        S.add("dve", TT(ei[:], e1[:], e2[:], ALU.add), reads=["e1", "e2"], writes=["ei"])
        S.add("dve", CP(A8[:, 0, :], ar[:, 15, :]), reads=["ar"], writes=["A8"])
        S.add("dve", CP(A8[:, 1, :], ai[:, 15, :]), reads=["ai"], writes=["A8"])

        for s_ in range(8):
            S.add("dve", CP(erR[:, s_, :], er[:, 7 - s_, :]), reads=["er"], writes=["erR"])
            S.add("dve", CP(eiR[:, s_, :], ei[:, 7 - s_, :]), reads=["ei"], writes=["eiR"])
        GB = 8
        SH = [64, GB, 8, 16]

        def bw(w3):
            return w3.unsqueeze(2).to_broadcast(SH)

        def bcf(c3):
            return c3.rearrange("p s g -> p g s").unsqueeze(3).to_broadcast(SH)

        def v4(t4):
            return t4.rearrange("p g (s c) -> p g s c", c=16)

        for gb in range(64 // GB):
            G0 = gb * GB
            gs = slice(G0, G0 + GB)
            Bre, Bim = bw(Bm[:, 0, gs, :]), bw(Bm[:, 1, gs, :])
            Cre, Cim = bw(Cm[:, 0, gs, :]), bw(Cm[:, 1, gs, :])
            erb, eib = bcf(erR[:, :, gs]), bcf(eiR[:, :, gs])
            T = [t[:] for t in tt_]
            S.add("dve", TT(T[0], Bre, erb, ALU.mult), reads=["Bm", "erR"], writes=[("t", 0)])
            S.add("dve", TT(T[1], Bim, eib, ALU.mult), reads=["Bm", "eiR"], writes=[("t", 1)])
            S.add("dve", TT(v4(PS[:, :, 0, :]), T[0], T[1], ALU.subtract), reads=[("t", 0), ("t", 1)], writes=["PS"])
            S.add("pool", TT(T[2], Bim, erb, ALU.mult), reads=["Bm", "erR"], writes=[("t", 2)])
            S.add("pool", TT(T[3], Bre, eib, ALU.mult), reads=["Bm", "eiR"], writes=[("t", 3)])
            S.add("pool", TT(v4(PS[:, :, 1, :]), T[2], T[3], ALU.add), reads=[("t", 2), ("t", 3)], writes=["PSi"])
            for dst, dk, lo in ((Q2, "Q2", 0), (Qf, "Qf", 8)):
                arb, aib = bcf(ar[:, lo:lo + 8, gs]), bcf(ai[:, lo:lo + 8, gs])
                S.add("dve", TT(T[0], Cre, arb, ALU.mult), reads=["Cm", "ar"], writes=[("t", 0)])
                S.add("dve", TT(T[1], Cim, aib, ALU.mult), reads=["Cm", "ai"], writes=[("t", 1)])
                S.add("dve", TT(v4(dst[:, :, 0, :]), T[0], T[1], ALU.subtract), reads=[("t", 0), ("t", 1)], writes=[dk])
                S.add("pool", TT(T[2], Cre, aib, ALU.mult), reads=["Cm", "ai"], writes=[("t", 2)])
                S.add("pool", TT(T[3], Cim, arb, ALU.mult), reads=["Cm", "ar"], writes=[("t", 3)])
                S.add("dve", STT(v4(dst[:, :, 1, :]), T[2], -1.0, T[3], ALU.mult, ALU.subtract),
                      reads=[("t", 2), ("t", 3)], writes=[dk + "i"])
            S.add("act", ACT(Qb[:, gs, :, :], Qf[:], AF.Copy), reads=["Qf", "Qfi"], writes=["Qb"])
            for g4 in range(GB // 4):
                bank = g4 % 2
                for q in range(4):
                    gl = g4 * 4 + q
                    col = q * 128
                    S.add("pe", MM(self.ps[bank][:, col:col + 128], PS[:, gl, 0, :], Q2[:, gl, 0, :], True, False),
                          reads=["PS", "PSi", "Q2", "Q2i"], writes=[("ps", bank)])
                    S.add("pe", MM(self.ps[bank][:, col:col + 128], PS[:, gl, 1, :], Q2[:, gl, 1, :], False, True),
                          reads=["PS", "PSi", "Q2", "Q2i"], writes=[("ps", bank)])
                S.add("dve", TT(mtmp[:], self.ps[bank][:].rearrange("p (a b) -> p a b", b=128),
                                self.s5mask.unsqueeze(1).to_broadcast([128, 4, 128]), ALU.mult),
                      reads=[("ps", bank), "consts"], writes=["mtmp"])
                for q in range(4):
                    g = G0 + g4 * 4 + q
                    S.add("dve", STT(M_T[:, g, :], self.ident, dcol[:, g:g + 1], mtmp[:, q, :], ALU.mult, ALU.add),
                          reads=["mtmp", "dcol", "consts"], writes=["M_T"])
                bank2 = 2 + g4 % 2
                for q in range(4):
                    gl = g4 * 4 + q
                    for ri in range(2):
                        col = q * 128 + ri * 64
                        S.add("pe", TR(self.ps[bank2][:, col:col + 64], PS[:, gl, ri, :], self.ident[0:64, 0:64]),
                              reads=["PS", "PSi", "consts"], writes=[("ps", bank2)])
                g0 = G0 + g4 * 4
                S.add("act", ACT(P_T[:, g0:g0 + 4, :], self.ps[bank2][:].rearrange("p (a b) -> p a b", b=128), AF.Copy),
                      reads=[("ps", bank2)], writes=["P_T"])

        S.barrier()
        A.reset(mark_main)
        r1 = A.mark()
        hs = [A.alloc([128, 4, NB], F32, "s_hs") for _ in range(2)]
        hn = A.alloc([128, NDT, NB], BF16, "s_hn")
        r2 = A.mark()
        wsl = [A.alloc([128, NDT, 512], BF16, "s_wsl") for _ in range(2)]
        r3 = A.mark()
        A.reset(r1)
        W = A.alloc([64, 2, 64, J], F32, "s_W")
        assert A.mark() <= r2
        A.reset(r2)
        U = A.alloc([128, 64, J], BF16, "s_U")
        Xbf = A.alloc([64, 2, 64, J], BF16, "s_Xbf")
        Yg = A.alloc([128, 64, J], BF16, "s_Yg")
        assert A.mark() <= r3
        A.reset(r3)
        uT = A.alloc([128, 8, NB], BF16, "s_uT")
        yT = A.alloc([128, 8, NB], BF16, "s_yT")
        ytmp = [A.alloc([128, 512], F32, "s_ytmp") for _ in range(2)]
        gtmp = [A.alloc([128, 512], F32, "s_gtmp") for _ in range(2)]
        sigt = [A.alloc([128, 512], F32, "s_sig") for _ in range(2)]
        P1 = A.alloc([64, 2, 64], F32, "s_P1")
        P2 = A.alloc([64, 2, 64], F32, "s_P2")
        Xn = A.alloc([64, 2, 64], F32, "s_Xn")
        A8r = A8[:, 0, :].unsqueeze(1).to_broadcast([64, 2, 64])
        A8i = A8[:, 1, :].unsqueeze(1).to_broadcast([64, 2, 64])
        w_in = self.s5_w_in
        w_out = self.s5_w_out
        nslab = 0
        nev = 0
        nblk = self.Lc // NB
        PAIRS = [[0, 1], [2, 3], [4, 5], [6, 7]]
        for blk in range(2 * nblk if self.hand else nblk):
            pre = self.hand and blk < nblk
            if self.hand and blk == nblk:
                S.add("sp", DMA(self.xs_src, X[:].rearrange("p a b -> p (a b)")), reads=["X"], writes=["xs_src"], dma=True)
                S.add("pool", (lambda e: e.collective_compute("AllGather", ALU.bypass, replica_groups=PAIRS,
                                                              ins=[self.xs_src], outs=[self.xs_dst])),
                      reads=["xs_src"], writes=["xs_dst"], cc=True)
                S.add("sp", DMA(X[:].rearrange("p a b -> p (a b)"), self.xs_dst[0:64, :]),
                      reads=["xs_dst"], writes=["X"], dma=True)
                S.add("dve", TS(X[:], X[:], self.flag[0:64, 0:1], None, ALU.mult), reads=["X", "flag"], writes=["X"])
            blk = blk % nblk
            t0 = blk * NB
            self.rmsnorm_stream(t0, 2 + j, hn, hs)
            for sl_ in range(2):
                b = nslab % 2
                nslab += 1
                S.add("pool", DMA(wsl[b][:], w_in[j, :, sl_ * 512:(sl_ + 1) * 512]
                                  .rearrange("(kt p) m -> p kt m", p=128)), writes=[("wsl", b)], dma=True)
                for f4 in range(4):
                    bank = 4 + f4 % 2
                    for kt in range(NDT):
                        S.add("pe", MM(self.ps[bank][:], wsl[b][:, kt, f4 * 128:(f4 + 1) * 128], hn[:, kt, :],
                                       kt == 0, kt == NDT - 1),
                              reads=[("wsl", b), ("hn", kt, 0)], writes=[("ps", bank)])
                    S.add("act", ACT(uT[:, sl_ * 4 + f4, :], self.ps[bank][:], AF.Copy),
                          reads=[("ps", bank)], writes=[("uT", sl_ * 4 + f4)])
            S.barrier()
            for ft in range(8):
                bank = ft % 2
                for gl in range(8):
                    for s_ in range(8):
                        S.add("pe", MM(self.ps[bank][:, gl * J:(gl + 1) * J], E[:, gl * 8 + s_, :],
                                       uT[:, ft, s_:NB:8], s_ == 0, s_ == 7),
                              reads=["E", ("uT", ft)], writes=[("ps", bank)])
                S.add("act" if ft % 2 == 0 else "dve",
                      (ACT(U[:, ft * 8:(ft + 1) * 8, :], self.ps[bank][:].rearrange("p (a b) -> p a b", b=J), AF.Copy)
                       if ft % 2 == 0 else
                       CP(U[:, ft * 8:(ft + 1) * 8, :], self.ps[bank][:].rearrange("p (a b) -> p a b", b=J))),
                      reads=[("ps", bank)], writes=[("U", ft)])
            for g8 in range(8):
                for ri in range(2):
                    bank = 2 + ri
                    for q in range(8):
                        g = g8 * 8 + q
                        S.add("pe", MM(self.ps[bank][0:64, q * J:(q + 1) * J], P_T[:, g, ri * 64:(ri + 1) * 64],
                                       U[:, g, :], True, True),
                              reads=["P_T", ("U", g8)], writes=[("ps", bank)])
                    S.add("dve" if ri == 0 else "act",
                          (CP(W[:, ri, g8 * 8:(g8 + 1) * 8, :], self.ps[bank][0:64, :].rearrange("p (a b) -> p a b", b=J))
                           if ri == 0 else
                           ACT(W[:, ri, g8 * 8:(g8 + 1) * 8, :], self.ps[bank][0:64, :].rearrange("p (a b) -> p a b", b=J), AF.Copy)),
                          reads=[("ps", bank)], writes=[("W", ri, g8)])
            WK = [("W", ri, g8) for ri in range(2) for g8 in range(8)]
            for jj in range(J):
                if not pre:
                    S.add("act", ACT(Xbf[:, :, :, jj], X[:], AF.Copy), reads=["X"], writes=["Xbf"])
                S.add("dve", TT(P1[:], X[:], A8r, ALU.mult), reads=["X", "A8"], writes=["P1"])
                S.add("dve", TT(P2[:], X[:], A8i, ALU.mult), reads=["X", "A8"], writes=["P2"])
                S.add("dve", TT(Xn[:, 0, :], P1[:, 0, :], P2[:, 1, :], ALU.subtract), reads=["P1", "P2"], writes=["Xn0"])
                S.add("dve", TT(Xn[:, 1, :], P1[:, 1, :], P2[:, 0, :], ALU.add), reads=["P1", "P2"], writes=["Xn1"])
                S.add("dve", TT(X[:], Xn[:], W[:, :, :, jj], ALU.add), reads=["Xn0", "Xn1"] + WK, writes=["X"])
            if pre:
                S.barrier()
                continue
            for g8 in range(8):
                bank = 4 + g8 % 2
                for q in range(8):
                    g = g8 * 8 + q
                    o = self.ps[bank][:, q * J:(q + 1) * J]
                    S.add("pe", MM(o, M_T[:, g, :], U[:, g, :], True, False),
                          reads=["M_T", ("U", g8)], writes=[("ps", bank)])
                    S.add("pe", MM(o, Qb[:, g, 0, :], Xbf[:, 0, g, :], False, False),
                          reads=["Qb", "Xbf"], writes=[("ps", bank)])
                    S.add("pe", MM(o, Qb[:, g, 1, :], Xbf[:, 1, g, :], False, True),
                          reads=["Qb", "Xbf"], writes=[("ps", bank)])
                e = nev % 2
                nev += 1
                xg, g2 = ytmp[e], gtmp[e]
                S.add("act", ACT(xg[:], self.ps[bank][:], AF.Copy), reads=[("ps", bank)], writes=[("xg", e)])
                S.add("dve", TT(g2[:], xg[:], xg[:], ALU.mult), reads=[("xg", e)], writes=[("g2", e)])
                S.add("dve", TS(g2[:], g2[:], 0.044715, 1.0, ALU.mult, ALU.add), reads=[("g2", e)], writes=[("g2", e)])
                S.add("dve", TT(g2[:], g2[:], xg[:], ALU.mult), reads=[("g2", e), ("xg", e)], writes=[("g2", e)])
                S.add("act", ACT(g2[:], g2[:], AF.Sigmoid, scale=1.5957691216057308),
                      reads=[("g2", e)], writes=[("g2", e)])
                S.add("dve", TT(Yg[:, g8 * 8:(g8 + 1) * 8, :].rearrange("p a b -> p (a b)"), xg[:], g2[:], ALU.mult),
                      reads=[("xg", e), ("g2", e)], writes=[("Yg", g8)])
            for ft in range(8):
                bank = 6 + ft % 2
                for t_ in range(8):
                    for gl in range(8):
                        S.add("pe", MM(self.ps[bank][:, t_:NB:8], E[:, t_ * 8 + gl, :], Yg[:, ft * 8 + gl, :],
                                       gl == 0, gl == 7),
                              reads=["E", ("Yg", ft)], writes=[("ps", bank)])
                S.add("act" if ft % 2 == 0 else "dve",
                      (ACT(yT[:, ft, :], self.ps[bank][:], AF.Copy) if ft % 2 == 0 else CP(yT[:, ft, :], self.ps[bank][:])),
                      reads=[("ps", bank)], writes=[("yT", ft)])
            S.barrier()
            src = self.res_src()
            for sl_ in range(4):
                b = nslab % 2
                nslab += 1
                S.add("pool", DMA(wsl[b][:, 0:8, :], w_out[j, :, sl_ * 512:(sl_ + 1) * 512]
                                  .rearrange("(kt p) m -> p kt m", p=128)), writes=[("wsl", b)], dma=True)
                S.add("pool", DMA(wsl[b][:, 8:16, :], w_out[j, :, D + sl_ * 512:D + (sl_ + 1) * 512]
                                  .rearrange("(kt p) m -> p kt m", p=128)), writes=[("wslg", b)], dma=True)
                hb_ = self.hs_n % 2
                self.hs_n += 1
                S.add("sp", DMA(hs[hb_][:], src[:, sl_ * 4:sl_ * 4 + 4, t0:t0 + NB]), writes=[("hs", hb_)], dma=True)
                for f4 in range(4):
                    bv, bg = 4 + f4 % 2, 6 + f4 % 2
                    for kt in range(8):
                        S.add("pe", MM(self.ps[bv][:], wsl[b][:, kt, f4 * 128:(f4 + 1) * 128], yT[:, kt, :],
                                       kt == 0, kt == 7),
                              reads=[("wsl", b), ("yT", kt)], writes=[("ps", bv)])
                    for kt in range(8):
                        S.add("pe", MM(self.ps[bg][:], wsl[b][:, 8 + kt, f4 * 128:(f4 + 1) * 128], yT[:, kt, :],
                                       kt == 0, kt == 7),
                              reads=[("wslg", b), ("yT", kt)], writes=[("ps", bg)])
                    sg = sigt[f4 % 2]
                    sk = ("sig", f4 % 2)
                    S.add("act", ACT(sg[:], self.ps[bg][:], AF.Sigmoid), reads=[("ps", bg)], writes=[sk])
                    S.add("dve", TT(sg[:], self.ps[bv][:], sg[:], ALU.mult), reads=[("ps", bv), sk], writes=[sk])
                    S.add("dve", TT(hs[hb_][:, f4, :], hs[hb_][:, f4, :], sg[:], ALU.add),
                          reads=[sk, ("hs", hb_)], writes=[("hs", hb_)])
                S.add("sp", DMA(self.hres[:, sl_ * 4:sl_ * 4 + 4, t0:t0 + NB], hs[hb_][:]),
                      reads=[("hs", hb_)], dma=True)
            S.barrier()
        self.first_res = False

    def final_phase(self, do_norm):
        S, A = self.S, self.A
        self.phase_begin()
        NB = 512
        hb = A.alloc([128, NDT, NB], F32, "f_h")
        ob = A.alloc([128, NDT, NB], F32, "f_o")
        for blk in range(self.Lc // NB):
            t0 = blk * NB
            self.load_h(hb, t0, NB, "h")
            if do_norm:
                self.rmsnorm(hb, ob, NB, 8, "h", "o")
                src, key = ob, "o"
            else:
                src, key = hb, "h"
            for q in range(4):
                S.add("sp", DMA(self.outT[:, 4 * q:4 * q + 4, t0:t0 + NB], src[:, 4 * q:4 * q + 4, :]),
                      reads=[(key, d_, 0) for d_ in range(4 * q, 4 * q + 4)], dma=True)


def make_consts():
    c = np.zeros((128, 5, 128), np.float32)
    c[:, 0, :] = np.eye(128, dtype=np.float32)
    s = np.arange(128)[:, None]
    t = np.arange(128)[None, :]
    c[:, 1, :] = np.where((s // 64 == t // 64) & (s > t), -1.0 / 16.0, 0.0)
    c[:, 2, :] = 1.0
    c[:, 3, :] = np.where((s // 16) <= (t // 16), 1.0, 0.0)
    c[:, 4, 0] = np.where(np.arange(128) < 64, -1.0 / 16.0, 0.0)
    c[:, 4, 1] = np.where(np.arange(128) >= 64, -1.0 / 16.0, 0.0)
    return c


def make_E():
    E = np.zeros((128, 64, 128), np.float32)
    for a in range(8):
        for b in range(8):
            for c in range(16):
                E[a * 16 + c, a * 8 + b, b * 16 + c] = 1.0
    return E


def col_layout(v):
    v = np.asarray(v, np.float32)
    lead = v.shape[:-1]
    r = v.reshape(*lead, NDT, 128)
    r = np.moveaxis(r, -1, 0)
    return np.ascontiguousarray(r)


def prep_shared(inp):
    f = lambda k: np.asarray(inp[k], np.float32)
    gains = np.zeros((128, 9, NDT), np.float32)
    gains[:, 0:2] = col_layout(f("gla_norm"))
    gains[:, 2:4] = col_layout(f("s5_norm"))
    gains[:, 4:8] = col_layout(f("mlp_norm"))
    gains[:, 8] = col_layout(f("final_norm"))
    wg = np.concatenate([f("gla_w_gate_up"), f("gla_b_gate")[:, None, :]], axis=1)
    on = np.ascontiguousarray(f("gla_o_norm").reshape(2, 4, 128).transpose(0, 2, 1))
    lam = np.stack([f("s5_lam_re").transpose(0, 2, 1), f("s5_lam_im").transpose(0, 2, 1),
                    np.broadcast_to(f("s5_log_dt")[:, None, :], (2, 64, 64))], axis=2)
    Bm = np.stack([f("s5_b_re").transpose(0, 2, 1, 3), f("s5_b_im").transpose(0, 2, 1, 3)], axis=2)
    Cm = np.stack([f("s5_c_re").transpose(0, 3, 1, 2), f("s5_c_im").transpose(0, 3, 1, 2)], axis=2)
    dsk = f("s5_d").reshape(2, 64, 16)
    dcol = np.broadcast_to(dsk.transpose(0, 2, 1)[:, None, :, :], (2, 8, 16, 64)).reshape(2, 128, 64)
    return {
        "gains": gains, "consts": make_consts(),
        "gla_w_in": f("gla_w_in"), "gla_wg": np.ascontiguousarray(wg), "gla_on": on,
        "gla_w_out": f("gla_w_out"),
        "s5_w_in": f("s5_w_in"), "s5_w_out": f("s5_w_out"),
        "s5_lam": np.ascontiguousarray(lam), "s5_B": np.ascontiguousarray(Bm),
        "s5_C": np.ascontiguousarray(Cm), "s5_dcol": np.ascontiguousarray(dcol),
        "s5_E": make_E(),
        "mlp_w_up": f("mlp_w_up"), "mlp_w_down": f("mlp_w_down"),
    }


def x_to_T(xs):
    L = xs.shape[0]
    return np.ascontiguousarray(xs.reshape(L, NDT, 128).transpose(2, 1, 0))


def T_to_x(oT):
    L = oT.shape[2]
    return np.ascontiguousarray(oT.transpose(2, 1, 0).reshape(L, D))


FULL_PHASES = []
for _i in range(DEPTH):
    FULL_PHASES.append(("gla" if _i % 2 == 0 else "s5", _i // 2))
    FULL_PHASES.append(("mlp", _i))

_CACHE = {}


def kernel(**inputs):
    x = np.asarray(inputs["x"], np.float32)
    shared = prep_shared(inputs)
    ncores = 8
    Lc = SEQ // 2
    key = (Lc, "full")
    if key not in _CACHE:
        _CACHE[key] = Builder(Lc, FULL_PHASES).nc
    nc = _CACHE[key]
    in_maps = []
    for c in range(ncores):
        b, half = c // 2, c % 2
        m = dict(shared)
        m["xT"] = x_to_T(x[b, half * Lc:(half + 1) * Lc])
        m["flag"] = np.full((128, 1), float(half), np.float32)
        in_maps.append(m)
    res = run_bass_kernel_spmd(nc, in_maps, core_ids=list(range(ncores)))
    out = np.zeros((BATCH, SEQ, D), np.float32)
    for c in range(ncores):
        b, half = c // 2, c % 2
        out[b, half * Lc:(half + 1) * Lc] = T_to_x(res.results[c]["outT"])
    return out
```

```python
import math
import contextlib
import numpy as np
import concourse.bass as bass
import concourse.mybir as mybir
from concourse.bass_utils import run_bass_kernel_spmd

F32 = mybir.dt.float32
BF16 = mybir.dt.bfloat16
ALU = mybir.AluOpType
AF = mybir.ActivationFunctionType

D = 2048
NDT = 16
SEQ = 2048
BATCH = 4
DEPTH = 4
EPS = 1e-6
HID = 8192
GLA_IN = 6160
DK = 256
DV = 512

ENGS = ["pe", "act", "dve", "pool", "sp"]
NDMA = 24


class Op:
    __slots__ = ("eng", "fn", "deps", "needed", "count", "is_dma", "slot",
                 "val", "wait_only", "is_cc")

    def __init__(self, eng, fn, is_dma=False):
        self.eng = eng
        self.fn = fn
        self.deps = []
        self.needed = False
        self.count = 0
        self.is_dma = is_dma
        self.slot = -1
        self.val = 0
        self.wait_only = False
        self.is_cc = False


class Sched:
    def __init__(self):
        self.ops = {e: [] for e in ENGS}
        self.last_w = {}
        self.readers = {}
        self.dma_rr = 0
        self.dma_last = [None] * NDMA
        self.dma_cnt = [0] * NDMA
        self.cc_last = None
        self.cc_cnt = 0

    def add(self, eng, fn, reads=(), writes=(), dma=False, cc=False):
        op = Op(eng, fn, dma or cc)
        op.is_cc = cc
        if eng != "pe":
            extra = [k for k in reads if isinstance(k, tuple) and k[0] == "ps" and k not in writes]
            if extra:
                writes = list(writes) + extra
        deps = {}
        for k in reads:
            w = self.last_w.get(k)
            if w is not None:
                deps[w] = deps.get(w, 0) | 1
        for k in writes:
            w = self.last_w.get(k)
            if w is not None:
                deps[w] = deps.get(w, 0) | 2
            for r in self.readers.get(k, ()):
                deps[r] = deps.get(r, 0) | 4
        if cc:
            if self.cc_last is not None:
                deps[self.cc_last] = deps.get(self.cc_last, 0) | 2
            self.cc_last = op
            self.cc_cnt += 1
            op.val = self.cc_cnt
        if dma:
            slot = self.dma_rr % NDMA
            self.dma_rr += 1
            prev = self.dma_last[slot]
            if prev is not None:
                deps[prev] = deps.get(prev, 0) | 2
            self.dma_last[slot] = op
            self.dma_cnt[slot] += 1
            op.slot = slot
            op.val = 16 * self.dma_cnt[slot]
        for d, kind in deps.items():
            if d is op:
                continue
            if (not d.is_dma) and d.eng == eng and not (dma or cc):
                if eng == "pe":
                    continue
                if not (kind & 3):
                    continue
            op.deps.append(d)
            d.needed = True
        for k in reads:
            self.readers.setdefault(k, []).append(op)
        for k in writes:
            self.last_w[k] = op
            self.readers[k] = []
        self.ops[eng].append(op)
        return op

    def barrier(self):
        lasts = []
        for e in ENGS:
            for op in reversed(self.ops[e]):
                if not op.is_dma and not op.wait_only:
                    lasts.append(op)
                    break
        dmas = [d for d in self.dma_last if d is not None]
        if self.cc_last is not None:
            dmas.append(self.cc_last)
        for e in ENGS:
            w = Op(e, None)
            w.wait_only = True
            for d in lasts + dmas:
                if d.eng == e and not d.is_dma:
                    continue
                w.deps.append(d)
                d.needed = True
            self.ops[e].append(w)
        self.last_w = {}
        self.readers = {}

    def emit(self, nc, engsems, dmasems, ccsem=None):
        for e in ENGS:
            c = 0
            for op in self.ops[e]:
                if op.is_dma or op.wait_only:
                    continue
                if op.needed:
                    c += 1
                    op.count = c
        sched = self

        def event(d):
            if d.is_cc:
                return ccsem, d.val
            if d.is_dma:
                return dmasems[d.slot], d.val
            return engsems[d.eng], d.count

        def run(e, eng):
            known = {}
            for op in sched.ops[e]:
                for d in op.deps:
                    sem, val = event(d)
                    if known.get(id(sem), 0) < val:
                        eng.wait_ge(sem, val)
                        known[id(sem)] = val
                if op.wait_only:
                    continue
                ins = op.fn(eng)
                if op.is_cc:
                    ins.then_inc(ccsem, 1)
                elif op.is_dma:
                    ins.then_inc(dmasems[op.slot], 16)
                elif op.needed:
                    ins.then_inc(engsems[e], 1)

        with nc.Block() as block:
            @block.tensor
            def _(eng):
                run("pe", eng)

            @block.scalar
            def _(eng):
                run("act", eng)

            @block.vector
            def _(eng):
                run("dve", eng)

            @block.gpsimd
            def _(eng):
                run("pool", eng)

            @block.sync
            def _(eng):
                run("sp", eng)


def MM(out, lhsT, rhs, start, stop):
    return lambda e: e.matmul(out, lhsT=lhsT, rhs=rhs, start=start, stop=stop)


def TR(out, in_, ident):
    return lambda e: e.transpose(out, in_, ident)


def ACT(out, in_, func, bias=None, scale=None):
    kw = {}
    if bias is not None:
        kw["bias"] = bias
    if scale is not None:
        kw["scale"] = scale
    return lambda e: e.activation(out=out, in_=in_, func=func, **kw)


def TT(out, in0, in1, op):
    return lambda e: e.tensor_tensor(out=out, in0=in0, in1=in1, op=op)


def STT(out, in0, scalar, in1, op0, op1):
    return lambda e: e.scalar_tensor_tensor(out=out, in0=in0, scalar=scalar,
                                            in1=in1, op0=op0, op1=op1)


def TS(out, in0, s1, s2, op0, op1=None):
    if op1 is None:
        return lambda e: e.tensor_scalar(out=out, in0=in0, scalar1=s1,
                                         scalar2=s2, op0=op0)
    return lambda e: e.tensor_scalar(out=out, in0=in0, scalar1=s1, scalar2=s2,
                                     op0=op0, op1=op1)


def CP(out, in_):
    return lambda e: e.tensor_copy(out=out, in_=in_)


def RCP(out, in_):
    return lambda e: e.reciprocal(out=out, in_=in_)


def SQRT(out, in_):
    return lambda e: e.sqrt(out=out, in_=in_)


def MSET(ap, v):
    return lambda e: e.memset(ap, v)


def DMA(out, in_):
    return lambda e: e.dma_start(out=out, in_=in_)


class Arena:
    def __init__(self, nc, lo=16512, hi=229376):
        self.nc = nc
        self.lo = lo
        self.hi = hi
        self.p = lo
        self.n = 0

    def alloc(self, shape, dtype, name="t"):
        esz = 2 if dtype == BF16 else 4
        size = esz
        for s in shape[1:]:
            size *= s
        size = (size + 63) // 64 * 64
        off = self.p
        self.p += size
        assert self.p <= self.hi, f"SBUF overflow {name} {self.p}"
        self.n += 1
        return self.nc.alloc_sbuf_tensor_at(f"{name}_{self.n}", list(shape),
                                            dtype, offset=off)

    def mark(self):
        return self.p

    def reset(self, m):
        self.p = m


class Builder:
    def __init__(self, Lc, phases, final_norm=True, hand=None):
        self.Lc = Lc
        self.nc = nc = bass.Bass("TRN2", target_bir_lowering=False)
        self.S = Sched()
        self.A = Arena(nc)
        dt = nc.dram_tensor
        self.xT = dt("xT", [128, NDT, Lc], F32, kind="ExternalInput").ap()
        self.outT = dt("outT", [128, NDT, Lc], F32, kind="ExternalOutput").ap()
        self.hres = dt("hres", [128, NDT, Lc], F32).ap()
        self.gains_d = dt("gains", [128, 9, NDT], F32, kind="ExternalInput").ap()
        self.consts_d = dt("consts", [128, 5, 128], F32, kind="ExternalInput").ap()
        self.gla_w_in = dt("gla_w_in", [2, D, GLA_IN], F32, kind="ExternalInput").ap()
        self.gla_wg = dt("gla_wg", [2, 17, 1024], F32, kind="ExternalInput").ap()
        self.gla_on = dt("gla_on", [2, 128, 4], F32, kind="ExternalInput").ap()
        self.gla_w_out = dt("gla_w_out", [2, D, D], F32, kind="ExternalInput").ap()
        self.s5_w_in = dt("s5_w_in", [2, D, 1024], F32, kind="ExternalInput").ap()
        self.s5_w_out = dt("s5_w_out", [2, 1024, 2 * D], F32, kind="ExternalInput").ap()
        self.s5_lam = dt("s5_lam", [2, 64, 3, 64], F32, kind="ExternalInput").ap()
        self.s5_B = dt("s5_B", [2, 64, 2, 64, 16], F32, kind="ExternalInput").ap()
        self.s5_C = dt("s5_C", [2, 64, 2, 64, 16], F32, kind="ExternalInput").ap()
        self.s5_dcol = dt("s5_dcol", [2, 128, 64], F32, kind="ExternalInput").ap()
        self.s5_E = dt("s5_E", [128, 64, 128], F32, kind="ExternalInput").ap()
        self.hand = (Lc < SEQ) if hand is None else hand
        self.flag_d = dt("flag", [128, 1], F32, kind="ExternalInput").ap()
        self.gs_src = dt("gs_src", [128, 4096], F32).ap()
        self.gs_dst = dt("gs_dst", [256, 4096], F32).ap()
        self.xs_src = dt("xs_src", [64, 128], F32).ap()
        self.xs_dst = dt("xs_dst", [128, 128], F32).ap()
        self.mlp_w_up = dt("mlp_w_up", [DEPTH, D, HID], F32, kind="ExternalInput").ap()
        self.mlp_w_down = dt("mlp_w_down", [DEPTH, HID, D], F32, kind="ExternalInput").ap()

        self.ps = [nc.alloc_psum_tensor(f"ps{i}", [128, 512], F32) for i in range(8)]
        A = self.A
        self.gains = A.alloc([128, 9, NDT], F32, "gains")
        self.consts = A.alloc([128, 5, 128], F32, "consts")
        self.sq = [A.alloc([128, 512], F32, "sq") for _ in range(2)]
        self.rstd = A.alloc([128, 512], F32, "rstd")
        self.flag = A.alloc([128, 1], F32, "flag")
        self.persist_mark = A.mark()
        S = self.S
        S.add("sp", DMA(self.gains[:], self.gains_d), writes=["gains"], dma=True)
        S.add("sp", DMA(self.consts[:], self.consts_d), writes=["consts"], dma=True)
        S.add("sp", DMA(self.flag[:], self.flag_d), writes=["flag"], dma=True)
        self.ident = self.consts[:, 0, :]
        self.tri2 = self.consts[:, 1, :]
        self.ones = self.consts[:, 2, :]
        self.s5mask = self.consts[:, 3, :]
        self.ind = self.consts[:, 4, 0:2]
        self.first_res = True
        self.hs_n = 0

        for ph in phases:
            kind = ph[0]
            if kind == "gla":
                self.gla_phase(ph[1])
            elif kind == "s5":
                self.s5_phase(ph[1])
            elif kind == "mlp":
                self.mlp_phase(ph[1])
        self.final_phase(final_norm)
        S.barrier()
        with contextlib.ExitStack() as st:
            engsems = {e: st.enter_context(nc.semaphore(f"s_{e}")) for e in ENGS}
            dmasems = [st.enter_context(nc.semaphore(f"d_{i}")) for i in range(NDMA)]
            ccsem = st.enter_context(nc.semaphore("ccsem"))
            S.emit(nc, engsems, dmasems, ccsem)

    def res_src(self):
        return self.xT if self.first_res else self.hres

    def phase_begin(self):
        self.S.barrier()
        self.A.reset(self.persist_mark)

    def load_h(self, hb, t0, ntok, key):
        src = self.res_src()
        ntt = ntok // 512
        for q in range(4):
            self.S.add("sp", DMA(hb[:, 4 * q:4 * q + 4, :], src[:, 4 * q:4 * q + 4, t0:t0 + ntok]),
                       writes=[(key, d_, tt) for d_ in range(4 * q, 4 * q + 4) for tt in range(ntt)],
                       dma=True)

    def rmsnorm(self, hb, hn, ntok, grow, hkey, nkey):
        S = self.S
        for tt in range(ntok // 512):
            sl = slice(tt * 512, (tt + 1) * 512)
            ps = self.ps[7]
            for d_ in range(NDT):
                sq = self.sq[d_ % 2]
                S.add("act", ACT(sq[:], hb[:, d_, sl], AF.Square),
                      reads=[(hkey, d_, tt)], writes=[("sq", d_ % 2)])
                S.add("pe", MM(ps[:], self.ones, sq[:], d_ == 0, d_ == NDT - 1),
                      reads=[("sq", d_ % 2), "consts"], writes=[("ps", 7)])
            S.add("dve", TS(self.rstd[:], ps[:], 1.0 / D, EPS, ALU.mult, ALU.add),
                  reads=[("ps", 7)], writes=["rstd"])
            S.add("act", SQRT(self.rstd[:], self.rstd[:]), reads=["rstd"], writes=["rstd"])
            S.add("dve", RCP(self.rstd[:], self.rstd[:]), reads=["rstd"], writes=["rstd"])
            for d_ in range(NDT):
                eng = "dve"
                S.add(eng, STT(hn[:, d_, sl], hb[:, d_, sl], self.gains[:, grow, d_:d_ + 1],
                               self.rstd[:], ALU.mult, ALU.mult),
                      reads=[(hkey, d_, tt), "rstd", "gains"], writes=[(nkey, d_, tt)])

    def rmsnorm_stream(self, t0, grow, hn, hs, nkey="hn"):
        S = self.S
        src = self.res_src()
        ps = self.ps[7]
        for q in range(4):
            b = self.hs_n % 2
            self.hs_n += 1
            S.add("sp", DMA(hs[b][:], src[:, 4 * q:4 * q + 4, t0:t0 + 512]), writes=[("hs", b)], dma=True)
            for r in range(4):
                d_ = 4 * q + r
                sq = self.sq[d_ % 2]
                S.add("act", ACT(sq[:], hs[b][:, r, :], AF.Square), reads=[("hs", b)], writes=[("sq", d_ % 2)])
                S.add("pe", MM(ps[:], self.ones, sq[:], d_ == 0, d_ == NDT - 1),
                      reads=[("sq", d_ % 2), "consts"], writes=[("ps", 7)])
        S.add("dve", TS(self.rstd[:], ps[:], 1.0 / D, EPS, ALU.mult, ALU.add),
              reads=[("ps", 7)], writes=["rstd"])
        S.add("act", SQRT(self.rstd[:], self.rstd[:]), reads=["rstd"], writes=["rstd"])
        S.add("dve", RCP(self.rstd[:], self.rstd[:]), reads=["rstd"], writes=["rstd"])
        for q in range(4):
            b = self.hs_n % 2
            self.hs_n += 1
            S.add("sp", DMA(hs[b][:], src[:, 4 * q:4 * q + 4, t0:t0 + 512]), writes=[("hs", b)], dma=True)
            for r in range(4):
                d_ = 4 * q + r
                eng = "dve"
                S.add(eng, STT(hn[:, d_, :], hs[b][:, r, :], self.gains[:, grow, d_:d_ + 1],
                               self.rstd[:], ALU.mult, ALU.mult),
                      reads=[("hs", b), "rstd", "gains"], writes=[(nkey, d_, 0)])

    def mlp_phase(self, layer):
        S, A = self.S, self.A
        MC = 512
        NMC = HID // MC
        for st in range(self.Lc // 1024):
            self.phase_begin()
            t0 = st * 1024
            hb = A.alloc([128, NDT, 1024], F32, "mlp_h")
            hn = A.alloc([128, NDT, 1024], BF16, "mlp_hn")
            wup = [A.alloc([128, NDT, MC], BF16, "wup") for _ in range(3)]
            wdn = [A.alloc([128, MC // 128, D], BF16, "wdn") for _ in range(2)]
            hid = [A.alloc([128, MC // 128, 1024], BF16, "hid") for _ in range(2)]
            tmp = [A.alloc([128, 512], F32, "rtmp") for _ in range(2)]
            w_up = self.mlp_w_up
            w_dn = self.mlp_w_down
            cnt = {"ev": 0, "dbank": 0}

            def ld_up(mc):
                if mc >= NMC:
                    return
                b3 = mc % 3
                S.add("pool", DMA(wup[b3][:], w_up[layer, :, mc * MC:(mc + 1) * MC]
                                  .rearrange("(kt p) m -> p kt m", p=128)),
                      writes=[("wup", b3)], dma=True)

            def ld_dn(mc):
                if mc >= NMC:
                    return
                b = mc % 2
                S.add("pool", DMA(wdn[b][:], w_dn[layer, mc * MC:(mc + 1) * MC, :]
                                  .rearrange("(kt p) d -> p kt d", p=128)),
                      writes=[("wdn", b)], dma=True)

            def up(mc):
                b = mc % 2
                b3 = mc % 3
                for mt in range(MC // 128):
                    pb = [self.ps[(mt % 2) * 2 + tt] for tt in range(2)]
                    pk = [("ps", (mt % 2) * 2 + tt) for tt in range(2)]
                    for kt in range(NDT):
                        for tt in range(2):
                            S.add("pe", MM(pb[tt][:], wup[b3][:, kt, mt * 128:(mt + 1) * 128],
                                           hn[:, kt, tt * 512:(tt + 1) * 512], kt == 0, kt == NDT - 1),
                                  reads=[("wup", b3), ("hn", kt, tt)], writes=[pk[tt]])
                    for tt in range(2):
                        ev = cnt["ev"]
                        tb = tmp[ev % 2]
                        tk = ("rtmp", ev % 2)
                        cnt["ev"] = ev + 1
                        S.add("act", ACT(tb[:], pb[tt][:], AF.Relu), reads=[pk[tt]], writes=[tk])
                        S.add("act", ACT(hid[b][:, mt, tt * 512:(tt + 1) * 512], tb[:], AF.Square),
                              reads=[tk], writes=[("hid", b, mt, tt)])

            def down(mc):
                b = mc % 2
                for d_ in range(NDT):
                    for tt in range(2):
                        bank = 4 + cnt["dbank"] % 3
                        cnt["dbank"] += 1
                        pbk = self.ps[bank]
                        nk = MC // 128
                        for j in range(nk):
                            S.add("pe", MM(pbk[:], wdn[b][:, j, d_ * 128:(d_ + 1) * 128],
                                           hid[b][:, j, tt * 512:(tt + 1) * 512], j == 0, j == nk - 1),
                                  reads=[("wdn", b), ("hid", b, j, tt)], writes=[("ps", bank)])
                        sl = slice(tt * 512, (tt + 1) * 512)
                        S.add("dve", TT(hb[:, d_, sl], pbk[:], hb[:, d_, sl], ALU.add),
                              reads=[("ps", bank), ("h", d_, tt)], writes=[("h", d_, tt)])

            ld_up(0)
            ld_dn(0)
            ld_up(1)
            self.load_h(hb, t0, 1024, "h")
            self.rmsnorm(hb, hn, 1024, 4 + layer, "h", "hn")
            up(0)
            for mc in range(NMC):
                ld_up(mc + 2)
                ld_dn(mc + 1)
                if mc + 1 < NMC:
                    up(mc + 1)
                down(mc)
            for q in range(4):
                S.add("sp", DMA(self.hres[:, 4 * q:4 * q + 4, t0:t0 + 1024], hb[:, 4 * q:4 * q + 4, :]),
                      reads=[("h", d_, tt) for d_ in range(4 * q, 4 * q + 4) for tt in range(2)],
                      dma=True)
        self.first_res = False

    def gla_phase(self, j):
        S, A = self.S, self.A
        NB = 512
        self.phase_begin()
        gla_S = A.alloc([128, 8, 512], F32, "glaS")
        hs = [A.alloc([128, 4, NB], F32, "g_hs") for _ in range(2)]
        hn = A.alloc([128, NDT, NB], BF16, "g_hn")
        wsl = [A.alloc([128, NDT, 512], BF16, "g_wsl") for _ in range(2)]
        gl = A.alloc([32, NB], F32, "g_gl")
        wg = A.alloc([32, 1024], F32, "g_wg")
        onc = A.alloc([128, 4], F32, "g_on")
        la = A.alloc([128, 1024], F32, "g_la")
        etmp = A.alloc([128, 1024], F32, "g_etmp")
        dec = A.alloc([128, 4, 1024], F32, "g_dec")
        cd = A.alloc([128, 8, 8], F32, "g_cd")
        kdec = A.alloc([128, 4, 1024], BF16, "g_kdec")
        vv = A.alloc([128, 4, 2048], BF16, "g_v")
        rT = A.alloc([128, 16, NB], BF16, "g_rT")
        qT = A.alloc([128, 8, NB], BF16, "g_qT")
        Sb = A.alloc([128, 8, 512], BF16, "g_Sb")
        oT = A.alloc([128, 16, NB], BF16, "g_oT")
        ssb = A.alloc([128, 4, NB], F32, "g_ssb")
        sqo = A.alloc([128, 512], F32, "g_sqo")
        utmp = A.alloc([128, 512], F32, "g_utmp")
        w_in = self.gla_w_in
        w_out = self.gla_w_out
        S.add("sp", DMA(wg[0:17, :], self.gla_wg[j]), writes=["wg"], dma=True)
        S.add("sp", DMA(onc[:], self.gla_on[j]), writes=["onc"], dma=True)
        S.add("pool", MSET(gl[:], 1.0), writes=["gl"])
        S.add("dve", MSET(gla_S[:], 0.0), writes=[("S", i) for i in range(8)])
        nslab = [0]
        rslab = [0]
        SSB = [("ssb", c) for c in range(8)]

        def load_slab(src_ap, ncols):
            b = nslab[0] % 2
            nslab[0] += 1
            S.add("pool", DMA(wsl[b][:, :, 0:ncols], src_ap.rearrange("(kt p) m -> p kt m", p=128)),
                  writes=[("wsl", b)], dma=True)
            return b

        nblk = self.Lc // NB
        PAIRS = [[0, 1], [2, 3], [4, 5], [6, 7]]
        for blk in range(2 * nblk if self.hand else nblk):
            pre = self.hand and blk < nblk
            if self.hand and blk == nblk:
                SK = [("S", i) for i in range(8)]
                S.add("sp", DMA(self.gs_src, gla_S[:].rearrange("p a b -> p (a b)")), reads=SK, writes=["gs_src"], dma=True)
                S.add("pool", (lambda e: e.collective_compute("AllGather", ALU.bypass, replica_groups=PAIRS,
                                                              ins=[self.gs_src], outs=[self.gs_dst])),
                      reads=["gs_src"], writes=["gs_dst"], cc=True)
                S.add("sp", DMA(gla_S[:].rearrange("p a b -> p (a b)"), self.gs_dst[0:128, :]),
                      reads=["gs_dst"], writes=SK, dma=True)
                for i in range(8):
                    S.add("dve", TS(gla_S[:, i, :], gla_S[:, i, :], self.flag[:, 0:1], None, ALU.mult),
                          reads=[("S", i), "flag"], writes=[("S", i)])
            blk = blk % nblk
            t0 = blk * NB
            self.rmsnorm_stream(t0, j, hn, hs)
            b = load_slab(w_in[j, :, 6144:6160], 16)
            for kt in range(NDT):
                S.add("pe", MM(self.ps[0][0:16, :], wsl[b][:, kt, 0:16], hn[:, kt, :], kt == 0, kt == NDT - 1),
                      reads=[("wsl", b), ("hn", kt, 0)], writes=[("ps", 0)])
            S.add("dve", CP(gl[0:16, :], self.ps[0][0:16, :]), reads=[("ps", 0)], writes=["gl"])
            for tt in range(4):
                for hf in range(2):
                    S.add("pe", MM(self.ps[1 + hf][:], gl[0:17, tt * 128:(tt + 1) * 128],
                                   wg[0:17, hf * 512:(hf + 1) * 512], True, True),
                          reads=["gl", "wg"], writes=[("ps", 1 + hf)])
                    S.add("act", ACT(etmp[:, hf * 512:(hf + 1) * 512], self.ps[1 + hf][:], AF.Exp, scale=-1.0),
                          reads=[("ps", 1 + hf)], writes=[("etmp", hf)])
                for hf in range(2):
                    S.add("act", ACT(la[:, hf * 512:(hf + 1) * 512], etmp[:, hf * 512:(hf + 1) * 512],
                                     AF.Ln, bias=1.0), reads=[("etmp", hf)], writes=[("la", hf)])
                for hf in range(2):
                    S.add("pe", MM(self.ps[1 + hf][:], self.tri2, la[:, hf * 512:(hf + 1) * 512], True, True),
                          reads=["consts", ("la", hf)], writes=[("ps", 1 + hf)])
                    S.add("act", ACT(dec[:, tt, hf * 512:(hf + 1) * 512], self.ps[1 + hf][:], AF.Exp),
                          reads=[("ps", 1 + hf)], writes=[("dec", tt, hf)])
                for ft in range(8):
                    S.add("pe", MM(self.ps[3][:, ft * 2:ft * 2 + 2], la[:, ft * 128:(ft + 1) * 128], self.ind, True, True),
                          reads=["consts", ("la", ft // 4)], writes=[("ps", 3)])
                S.add("act", ACT(cd[:, :, tt * 2:tt * 2 + 2],
                                 self.ps[3][:, 0:16].rearrange("p (a b) -> p a b", b=2), AF.Exp),
                      reads=[("ps", 3)], writes=[("cd", tt)])
            for sl_ in range(2):
                b = load_slab(w_in[j, :, 1024 + sl_ * 512:1024 + (sl_ + 1) * 512], 512)
                for tt in range(4):
                    bank = 4 + tt % 2
                    for kt in range(NDT):
                        S.add("pe", MM(self.ps[bank][:], hn[:, kt, tt * 128:(tt + 1) * 128], wsl[b][:, kt, :],
                                       kt == 0, kt == NDT - 1),
                              reads=[("wsl", b), ("hn", kt, 0)], writes=[("ps", bank)])
                    S.add("dve", TT(kdec[:, tt, sl_ * 512:(sl_ + 1) * 512], self.ps[bank][:],
                                    dec[:, tt, sl_ * 512:(sl_ + 1) * 512], ALU.mult),
                          reads=[("ps", bank), ("dec", tt, sl_)], writes=[("kdec", tt, sl_)])
            for sl_ in range(4):
                b = load_slab(w_in[j, :, 2048 + sl_ * 512:2048 + (sl_ + 1) * 512], 512)
                for tt in range(4):
                    bank = 4 + tt % 2
                    for kt in range(NDT):
                        S.add("pe", MM(self.ps[bank][:], hn[:, kt, tt * 128:(tt + 1) * 128], wsl[b][:, kt, :],
                                       kt == 0, kt == NDT - 1),
                              reads=[("wsl", b), ("hn", kt, 0)], writes=[("ps", bank)])
                    S.add("act", ACT(vv[:, tt, sl_ * 512:(sl_ + 1) * 512], self.ps[bank][:], AF.Copy),
                          reads=[("ps", bank)], writes=[("v", tt, sl_)])
            for sl_ in range(0 if pre else 2):
                b = load_slab(w_in[j, :, sl_ * 512:(sl_ + 1) * 512], 512)
                for f4 in range(4):
                    bank = 4 + f4 % 2
                    for kt in range(NDT):
                        S.add("pe", MM(self.ps[bank][:], wsl[b][:, kt, f4 * 128:(f4 + 1) * 128], hn[:, kt, :],
                                       kt == 0, kt == NDT - 1),
                              reads=[("wsl", b), ("hn", kt, 0)], writes=[("ps", bank)])
                    S.add("act", ACT(qT[:, sl_ * 4 + f4, :], self.ps[bank][:], AF.Copy, scale=DK ** -0.5),
                          reads=[("ps", bank)], writes=[("qT", sl_ * 4 + f4)])
            for c in range(8):
                tt, hf = c // 2, c % 2
                rows = slice(hf * 64, hf * 64 + 64)
                for h in range(4):
                    for kt in range(2):
                        i = h * 2 + kt
                        bank = i % 2
                        S.add("pe", MM(self.ps[bank][:], kdec[rows, tt, h * 256 + kt * 128:h * 256 + (kt + 1) * 128],
                                       vv[rows, tt, h * 512:(h + 1) * 512], True, True),
                              reads=[("kdec", tt, h // 2), ("v", tt, h)], writes=[("ps", bank)])
                        S.add("dve", STT(gla_S[:, i, :], gla_S[:, i, :], cd[:, i, c:c + 1],
                                         self.ps[bank][:], ALU.mult, ALU.add),
                              reads=[("S", i), ("cd", tt), ("ps", bank)], writes=[("S", i)])
                        S.add("act", ACT(Sb[:, i, :], gla_S[:, i, :], AF.Copy), reads=[("S", i)], writes=[("Sb", i)])
                if pre:
                    continue
                for i in (2 * c, 2 * c + 1):
                    sl_, f4 = i // 4, i % 4
                    if f4 == 0:
                        rslab[0] = load_slab(w_in[j, :, 4096 + sl_ * 512:4096 + (sl_ + 1) * 512], 512)
                    b = rslab[0]
                    bank = 4 + f4 % 2
                    for kt in range(NDT):
                        S.add("pe", MM(self.ps[bank][:], wsl[b][:, kt, f4 * 128:(f4 + 1) * 128], hn[:, kt, :],
                                       kt == 0, kt == NDT - 1),
                              reads=[("wsl", b), ("hn", kt, 0)], writes=[("ps", bank)])
                    S.add("act", ACT(rT[:, i, :], self.ps[bank][:], AF.Silu),
                          reads=[("ps", bank)], writes=[("rT", i)])
                for h2 in range(2):
                    bank = 2 + h2
                    for hh in range(2):
                        h = h2 * 2 + hh
                        for vt in range(4):
                            col = (hh * 4 + vt) * 64
                            for kt in range(2):
                                i = h * 2 + kt
                                S.add("pe", MM(self.ps[bank][:, col:col + 64], Sb[:, i, vt * 128:(vt + 1) * 128],
                                               qT[:, i, c * 64:(c + 1) * 64], kt == 0, kt == 1),
                                      reads=[("Sb", i), ("qT", i)], writes=[("ps", bank)])
                    S.add("dve", CP(oT[:, h2 * 8:h2 * 8 + 8, c * 64:(c + 1) * 64],
                                    self.ps[bank][:].rearrange("p (a b) -> p a b", b=64)),
                          reads=[("ps", bank)],
                          writes=[("oT", h2, c)] + [("gT", i) for i in range(h2 * 8, h2 * 8 + 8)])
                    S.add("act", ACT(sqo[:], self.ps[bank][:], AF.Square),
                          reads=[("ps", bank), ("oT", h2, c)], writes=["sqo"])
                    for hh in range(2):
                        for vt in range(4):
                            col = (hh * 4 + vt) * 64
                            S.add("pe", MM(self.ps[6][:, (h2 * 2 + hh) * 64:(h2 * 2 + hh) * 64 + 64], self.ones,
                                           sqo[:, col:col + 64], vt == 0, vt == 3),
                                  reads=["consts", "sqo"], writes=[("ps", 6)])
                S.add("dve", CP(ssb[:, :, c * 64:(c + 1) * 64],
                                self.ps[6][:, 0:256].rearrange("p (a b) -> p a b", b=64)),
                      reads=[("ps", 6)], writes=[("ssb", c)])
            if pre:
                continue
            S.add("dve", TS(ssb[:], ssb[:], 1.0 / DV, EPS, ALU.mult, ALU.add), reads=SSB, writes=SSB)
            S.add("act", SQRT(ssb[:], ssb[:]), reads=SSB, writes=SSB)
            S.add("dve", RCP(ssb[:], ssb[:]), reads=SSB, writes=SSB)
            for sl_ in range(4):
                for f4 in range(4):
                    i = sl_ * 4 + f4
                    S.add("dve", STT(utmp[:], oT[:, i, :], onc[:, f4:f4 + 1], ssb[:, sl_, :], ALU.mult, ALU.mult),
                          reads=[("oT", i // 8, c) for c in range(8)] + SSB + ["onc"],
                          writes=["utmp"])
                    S.add("dve", TT(oT[:, i, :], utmp[:], rT[:, i, :], ALU.mult),
                          reads=["utmp", ("rT", i)], writes=[("gT", i)])
            src = self.res_src()
            for sl_ in range(4):
                b = load_slab(w_out[j, :, sl_ * 512:(sl_ + 1) * 512], 512)
                hb_ = self.hs_n % 2
                self.hs_n += 1
                S.add("sp", DMA(hs[hb_][:], src[:, sl_ * 4:sl_ * 4 + 4, t0:t0 + NB]),
                      writes=[("hs", hb_)], dma=True)
                for f4 in range(4):
                    bank = 4 + f4 % 2
                    for kt in range(NDT):
                        S.add("pe", MM(self.ps[bank][:], wsl[b][:, kt, f4 * 128:(f4 + 1) * 128], oT[:, kt, :],
                                       kt == 0, kt == NDT - 1),
                              reads=[("wsl", b), ("gT", kt)], writes=[("ps", bank)])
                    S.add("dve", TT(hs[hb_][:, f4, :], self.ps[bank][:], hs[hb_][:, f4, :], ALU.add),
                          reads=[("ps", bank), ("hs", hb_)], writes=[("hs", hb_)])
                S.add("sp", DMA(self.hres[:, sl_ * 4:sl_ * 4 + 4, t0:t0 + NB], hs[hb_][:]),
                      reads=[("hs", hb_)], dma=True)
        self.first_res = False

    def s5_phase(self, j):
        S, A = self.S, self.A
        NB = 512
        J = NB // 8
        TWO_PI = 2.0 * math.pi
        I32 = mybir.dt.int32
        self.phase_begin()
        E = A.alloc([128, 64, 128], BF16, "s5E")
        M_T = A.alloc([128, 64, 128], BF16, "s5M")
        P_T = A.alloc([128, 64, 128], BF16, "s5P")
        Qb = A.alloc([64, 64, 2, 128], BF16, "s5Q")
        A8 = A.alloc([64, 2, 64], F32, "s5A8")
        X = A.alloc([64, 2, 64], F32, "s5X")
        mark_main = A.mark()
        lam = A.alloc([64, 3, 64], F32, "c_lam")
        Bm = A.alloc([64, 2, 64, 16], F32, "c_B")
        Cm = A.alloc([64, 2, 64, 16], F32, "c_C")
        dcol = A.alloc([128, 64], F32, "c_dcol")
        dtv = A.alloc([64, 64], F32, "c_dt")
        lr = A.alloc([64, 64], F32, "c_lr")
        xx = A.alloc([64, 64], F32, "c_xx")
        ang = A.alloc([64, 64], F32, "c_ang")
        mg = A.alloc([64, 16, 64], F32, "c_mg")
        ths = A.alloc([64, 16, 64], F32, "c_ths")
        thc = A.alloc([64, 16, 64], F32, "c_thc")
        ctmp = A.alloc([64, 16, 64], F32, "c_ctmp")
        ki = A.alloc([64, 16, 64], I32, "c_ki")
        ar = A.alloc([64, 16, 64], F32, "c_ar")
        ai = A.alloc([64, 16, 64], F32, "c_ai")
        er = A.alloc([64, 8, 64], F32, "c_er")
        ei = A.alloc([64, 8, 64], F32, "c_ei")
        e1 = A.alloc([64, 8, 64], F32, "c_e1")
        e2 = A.alloc([64, 8, 64], F32, "c_e2")
        fw = [A.alloc([64, 64], F32, "c_fw") for _ in range(6)]
        PS = A.alloc([64, 8, 2, 128], F32, "c_PS")
        Q2 = A.alloc([64, 8, 2, 128], F32, "c_Q2")
        Qf = A.alloc([64, 8, 2, 128], F32, "c_Qf")
        tt_ = [A.alloc([64, 8, 8, 16], F32, "c_t") for _ in range(4)]
        erR = A.alloc([64, 8, 64], F32, "c_erR")
        eiR = A.alloc([64, 8, 64], F32, "c_eiR")
        mtmp = A.alloc([128, 4, 128], F32, "c_mtmp")

        S.add("pool", DMA(E[:], self.s5_E), writes=["E"], dma=True)
        S.add("sp", DMA(lam[:], self.s5_lam[j]), writes=["lam"], dma=True)
        S.add("sp", DMA(Bm[:], self.s5_B[j]), writes=["Bm"], dma=True)
        S.add("sp", DMA(Cm[:], self.s5_C[j]), writes=["Cm"], dma=True)
        S.add("sp", DMA(dcol[:], self.s5_dcol[j]), writes=["dcol"], dma=True)
        S.add("dve", MSET(X[:], 0.0), writes=["X"])
        S.add("act", ACT(dtv[:], lam[:, 2, :], AF.Exp), reads=["lam"], writes=["dtv"])
        S.add("dve", TS(lr[:], lam[:, 0, :], -1e-4, None, ALU.min), reads=["lam"], writes=["lr"])
        S.add("dve", TT(xx[:], lr[:], dtv[:], ALU.mult), reads=["lr", "dtv"], writes=["xx"])
        S.add("dve", TT(ang[:], lam[:, 1, :], dtv[:], ALU.mult), reads=["lam", "dtv"], writes=["ang"])
        for ti, tau in enumerate(range(-7, 9)):
            S.add("act", ACT(mg[:, ti, :], xx[:], AF.Exp, scale=float(tau)), reads=["xx"], writes=["mg"])
            S.add("dve", TS(ths[:, ti, :], ang[:], float(tau), None, ALU.mult), reads=["ang"], writes=["ths"])
        S.add("dve", TS(thc[:], ths[:], math.pi / 2, None, ALU.add), reads=["ths"], writes=["thc"])
        for th, k in ((ths, "ths"), (thc, "thc")):
            S.add("dve", TS(ctmp[:], th[:], 1.0 / TWO_PI, None, ALU.mult), reads=[k], writes=["ctmp"])
            S.add("dve", CP(ki[:], ctmp[:]), reads=["ctmp"], writes=["ki"])
            S.add("dve", CP(ctmp[:], ki[:]), reads=["ki"], writes=["ctmp"])
            S.add("dve", STT(th[:], ctmp[:], -TWO_PI, th[:], ALU.mult, ALU.add), reads=["ctmp", k], writes=[k])
            S.add("dve", TS(ctmp[:], th[:], math.pi, TWO_PI, ALU.is_gt, ALU.mult), reads=[k], writes=["ctmp"])
            S.add("dve", TT(th[:], th[:], ctmp[:], ALU.subtract), reads=[k, "ctmp"], writes=[k])
            S.add("dve", TS(ctmp[:], th[:], -math.pi, TWO_PI, ALU.is_lt, ALU.mult), reads=[k], writes=["ctmp"])
            S.add("dve", TT(th[:], th[:], ctmp[:], ALU.add), reads=[k, "ctmp"], writes=[k])
            S.add("act", ACT(th[:], th[:], AF.Sin), reads=[k], writes=[k])
        S.add("dve", TT(ar[:], mg[:], thc[:], ALU.mult), reads=["mg", "thc"], writes=["ar"])
        S.add("dve", TT(ai[:], mg[:], ths[:], ALU.mult), reads=["mg", "ths"], writes=["ai"])
        nr, den, t1, t2, f_re, f_im = fw
        li = lam[:, 1, :]
        ni = ai[:, 8, :]
        S.add("dve", TS(nr[:], ar[:, 8, :], -1.0, None, ALU.add), reads=["ar"], writes=["nr"])
        S.add("dve", TT(t1[:], lr[:], lr[:], ALU.mult), reads=["lr"], writes=["t1"])
        S.add("dve", TT(t2[:], li, li, ALU.mult), reads=["lam"], writes=["t2"])
        S.add("dve", TT(den[:], t1[:], t2[:], ALU.add), reads=["t1", "t2"], writes=["den"])
        S.add("dve", RCP(den[:], den[:]), reads=["den"], writes=["den"])
        S.add("dve", TT(t1[:], nr[:], lr[:], ALU.mult), reads=["nr", "lr", "den"], writes=["t1"])
        S.add("dve", TT(t2[:], ni, li, ALU.mult), reads=["ai", "lam", "den"], writes=["t2"])
        S.add("dve", TT(t1[:], t1[:], t2[:], ALU.add), reads=["t1", "t2"], writes=["t1"])
        S.add("dve", TT(f_re[:], t1[:], den[:], ALU.mult), reads=["t1", "den"], writes=["f_re"])
        S.add("dve", TT(t1[:], ni, lr[:], ALU.mult), reads=["ai", "lr", "f_re"], writes=["t1"])
        S.add("dve", TT(t2[:], nr[:], li, ALU.mult), reads=["nr", "lam", "f_re"], writes=["t2"])
        S.add("dve", TT(t1[:], t1[:], t2[:], ALU.subtract), reads=["t1", "t2"], writes=["t1"])
        S.add("dve", TT(f_im[:], t1[:], den[:], ALU.mult), reads=["t1", "den"], writes=["f_im"])
        frb = f_re[:].unsqueeze(1).to_broadcast([64, 8, 64])
        fib = f_im[:].unsqueeze(1).to_broadcast([64, 8, 64])
        S.add("dve", TT(e1[:], ar[:, 7:15, :], frb, ALU.mult), reads=["ar", "f_re"], writes=["e1"])
        S.add("dve", TT(e2[:], ai[:, 7:15, :], fib, ALU.mult), reads=["ai", "f_im"], writes=["e2"])
        S.add("dve", TT(er[:], e1[:], e2[:], ALU.subtract), reads=["e1", "e2"], writes=["er"])
        S.add("dve", TT(e1[:], ar[:, 7:15, :], fib, ALU.mult), reads=["ar", "f_im", "er"], writes=["e1"])
        S.add("dve", TT(e2[:], ai[:, 7:15, :], frb, ALU.mult), reads=["ai", "f_re", "er"], writes=["e2"])
        S.add("dve", TT(ei[:], e1[:], e2[:], ALU.add), reads=["e1", "e2"], writes=["ei"])
        S.add("dve", CP(A8[:, 0, :], ar[:, 15, :]), reads=["ar"], writes=["A8"])
        S.add("dve", CP(A8[:, 1, :], ai[:, 15, :]), reads=["ai"], writes=["A8"])

        for s_ in range(8):
            S.add("dve", CP(erR[:, s_, :], er[:, 7 - s_, :]), reads=["er"], writes=["erR"])
            S.add("dve", CP(eiR[:, s_, :], ei[:, 7 - s_, :]), reads=["ei"], writes=["eiR"])
        GB = 8
        SH = [64, GB, 8, 16]

        def bw(w3):
            return w3.unsqueeze(2).to_broadcast(SH)

        def bcf(c3):
            return c3.rearrange("p s g -> p g s").unsqueeze(3).to_broadcast(SH)

        def v4(t4):
            return t4.rearrange("p g (s c) -> p g s c", c=16)

        for gb in range(64 // GB):
            G0 = gb * GB
            gs = slice(G0, G0 + GB)
            Bre, Bim = bw(Bm[:, 0, gs, :]), bw(Bm[:, 1, gs, :])
            Cre, Cim = bw(Cm[:, 0, gs, :]), bw(Cm[:, 1, gs, :])
            erb, eib = bcf(erR[:, :, gs]), bcf(eiR[:, :, gs])
            T = [t[:] for t in tt_]
            S.add("dve", TT(T[0], Bre, erb, ALU.mult), reads=["Bm", "erR"], writes=[("t", 0)])
            S.add("dve", TT(T[1], Bim, eib, ALU.mult), reads=["Bm", "eiR"], writes=[("t", 1)])
            S.add("dve", TT(v4(PS[:, :, 0, :]), T[0], T[1], ALU.subtract), reads=[("t", 0), ("t", 1)], writes=["PS"])
            S.add("pool", TT(T[2], Bim, erb, ALU.mult), reads=["Bm", "erR"], writes=[("t", 2)])
            S.add("pool", TT(T[3], Bre, eib, ALU.mult), reads=["Bm", "eiR"], writes=[("t", 3)])
            S.add("pool", TT(v4(PS[:, :, 1, :]), T[2], T[3], ALU.add), reads=[("t", 2), ("t", 3)], writes=["PSi"])
            for dst, dk, lo in ((Q2, "Q2", 0), (Qf, "Qf", 8)):
                arb, aib = bcf(ar[:, lo:lo + 8, gs]), bcf(ai[:, lo:lo + 8, gs])
                S.add("dve", TT(T[0], Cre, arb, ALU.mult), reads=["Cm", "ar"], writes=[("t", 0)])
                S.add("dve", TT(T[1], Cim, aib, ALU.mult), reads=["Cm", "ai"], writes=[("t", 1)])
                S.add("dve", TT(v4(dst[:, :, 0, :]), T[0], T[1], ALU.subtract), reads=[("t", 0), ("t", 1)], writes=[dk])
                S.add("pool", TT(T[2], Cre, aib, ALU.mult), reads=["Cm", "ai"], writes=[("t", 2)])
                S.add("pool", TT(T[3], Cim, arb, ALU.mult), reads=["Cm", "ar"], writes=[("t", 3)])
                S.add("dve", STT(v4(dst[:, :, 1, :]), T[2], -1.0, T[3], ALU.mult, ALU.subtract),
                      reads=[("t", 2), ("t", 3)], writes=[dk + "i"])
            S.add("act", ACT(Qb[:, gs, :, :], Qf[:], AF.Copy), reads=["Qf", "Qfi"], writes=["Qb"])
            for g4 in range(GB // 4):
                bank = g4 % 2
                for q in range(4):
                    gl = g4 * 4 + q
                    col = q * 128
                    S.add("pe", MM(self.ps[bank][:, col:col + 128], PS[:, gl, 0, :], Q2[:, gl, 0, :], True, False),
                          reads=["PS", "PSi", "Q2", "Q2i"], writes=[("ps", bank)])
                    S.add("pe", MM(self.ps[bank][:, col:col + 128], PS[:, gl, 1, :], Q2[:, gl, 1, :], False, True),
                          reads=["PS", "PSi", "Q2", "Q2i"], writes=[("ps", bank)])
                S.add("dve", TT(mtmp[:], self.ps[bank][:].rearrange("p (a b) -> p a b", b=128),
                                self.s5mask.unsqueeze(1).to_broadcast([128, 4, 128]), ALU.mult),
                      reads=[("ps", bank), "consts"], writes=["mtmp"])
                for q in range(4):
                    g = G0 + g4 * 4 + q
                    S.add("dve", STT(M_T[:, g, :], self.ident, dcol[:, g:g + 1], mtmp[:, q, :], ALU.mult, ALU.add),
                          reads=["mtmp", "dcol", "consts"], writes=["M_T"])
                bank2 = 2 + g4 % 2
                for q in range(4):
                    gl = g4 * 4 + q
                    for ri in range(2):
                        col = q * 128 + ri * 64
                        S.add("pe", TR(self.ps[bank2][:, col:col + 64], PS[:, gl, ri, :], self.ident[0:64, 0:64]),
                              reads=["PS", "PSi", "consts"], writes=[("ps", bank2)])
                g0 = G0 + g4 * 4
                S.add("act", ACT(P_T[:, g0:g0 + 4, :], self.ps[bank2][:].rearrange("p (a b) -> p a b", b=128), AF.Copy),
                      reads=[("ps", bank2)], writes=["P_T"])

        S.barrier()
        A.reset(mark_main)
        r1 = A.mark()
        hs = [A.alloc([128, 4, NB], F32, "s_hs") for _ in range(2)]
        hn = A.alloc([128, NDT, NB], BF16, "s_hn")
        r2 = A.mark()
        wsl = [A.alloc([128, NDT, 512], BF16, "s_wsl") for _ in range(2)]
        r3 = A.mark()
        A.reset(r1)
        W = A.alloc([64, 2, 64, J], F32, "s_W")
        assert A.mark() <= r2
        A.reset(r2)
        U = A.alloc([128, 64, J], BF16, "s_U")
        Xbf = A.alloc([64, 2, 64, J], BF16, "s_Xbf")
        Yg = A.alloc([128, 64, J], BF16, "s_Yg")
        assert A.mark() <= r3
        A.reset(r3)
        uT = A.alloc([128, 8, NB], BF16, "s_uT")
        yT = A.alloc([128, 8, NB], BF16, "s_yT")
        ytmp = [A.alloc([128, 512], F32, "s_ytmp") for _ in range(2)]
        gtmp = [A.alloc([128, 512], F32, "s_gtmp") for _ in range(2)]
        sigt = [A.alloc([128, 512], F32, "s_sig") for _ in range(2)]
        P1 = A.alloc([64, 2, 64], F32, "s_P1")
        P2 = A.alloc([64, 2, 64], F32, "s_P2")
        Xn = A.alloc([64, 2, 64], F32, "s_Xn")
        A8r = A8[:, 0, :].unsqueeze(1).to_broadcast([64, 2, 64])
        A8i = A8[:, 1, :].unsqueeze(1).to_broadcast([64, 2, 64])
        w_in = self.s5_w_in
        w_out = self.s5_w_out
        nslab = 0
        nev = 0
        nblk = self.Lc // NB
        PAIRS = [[0, 1], [2, 3], [4, 5], [6, 7]]
        for blk in range(2 * nblk if self.hand else nblk):
            pre = self.hand and blk < nblk
            if self.hand and blk == nblk:
                S.add("sp", DMA(self.xs_src, X[:].rearrange("p a b -> p (a b)")), reads=["X"], writes=["xs_src"], dma=True)
                S.add("pool", (lambda e: e.collective_compute("AllGather", ALU.bypass, replica_groups=PAIRS,
                                                              ins=[self.xs_src], outs=[self.xs_dst])),
                      reads=["xs_src"], writes=["xs_dst"], cc=True)
                S.add("sp", DMA(X[:].rearrange("p a b -> p (a b)"), self.xs_dst[0:64, :]),
                      reads=["xs_dst"], writes=["X"], dma=True)
                S.add("dve", TS(X[:], X[:], self.flag[0:64, 0:1], None, ALU.mult), reads=["X", "flag"], writes=["X"])
            blk = blk % nblk
            t0 = blk * NB
            self.rmsnorm_stream(t0, 2 + j, hn, hs)
            for sl_ in range(2):
                b = nslab % 2
                nslab += 1
                S.add("pool", DMA(wsl[b][:], w_in[j, :, sl_ * 512:(sl_ + 1) * 512]
                                  .rearrange("(kt p) m -> p kt m", p=128)), writes=[("wsl", b)], dma=True)
                for f4 in range(4):
                    bank = 4 + f4 % 2
                    for kt in range(NDT):
                        S.add("pe", MM(self.ps[bank][:], wsl[b][:, kt, f4 * 128:(f4 + 1) * 128], hn[:, kt, :],
                                       kt == 0, kt == NDT - 1),
                              reads=[("wsl", b), ("hn", kt, 0)], writes=[("ps", bank)])
                    S.add("act", ACT(uT[:, sl_ * 4 + f4, :], self.ps[bank][:], AF.Copy),
                          reads=[("ps", bank)], writes=[("uT", sl_ * 4 + f4)])
            S.barrier()
            for ft in range(8):
                bank = ft % 2
                for gl in range(8):
                    for s_ in range(8):
                        S.add("pe", MM(self.ps[bank][:, gl * J:(gl + 1) * J], E[:, gl * 8 + s_, :],
                                       uT[:, ft, s_:NB:8], s_ == 0, s_ == 7),
                              reads=["E", ("uT", ft)], writes=[("ps", bank)])
                S.add("act" if ft % 2 == 0 else "dve",
                      (ACT(U[:, ft * 8:(ft + 1) * 8, :], self.ps[bank][:].rearrange("p (a b) -> p a b", b=J), AF.Copy)
                       if ft % 2 == 0 else
                       CP(U[:, ft * 8:(ft + 1) * 8, :], self.ps[bank][:].rearrange("p (a b) -> p a b", b=J))),
                      reads=[("ps", bank)], writes=[("U", ft)])
            for g8 in range(8):
                for ri in range(2):
                    bank = 2 + ri
                    for q in range(8):
                        g = g8 * 8 + q
                        S.add("pe", MM(self.ps[bank][0:64, q * J:(q + 1) * J], P_T[:, g, ri * 64:(ri + 1) * 64],
                                       U[:, g, :], True, True),
                              reads=["P_T", ("U", g8)], writes=[("ps", bank)])
                    S.add("dve" if ri == 0 else "act",
                          (CP(W[:, ri, g8 * 8:(g8 + 1) * 8, :], self.ps[bank][0:64, :].rearrange("p (a b) -> p a b", b=J))
                           if ri == 0 else
                           ACT(W[:, ri, g8 * 8:(g8 + 1) * 8, :], self.ps[bank][0:64, :].rearrange("p (a b) -> p a b", b=J), AF.Copy)),
                          reads=[("ps", bank)], writes=[("W", ri, g8)])
            WK = [("W", ri, g8) for ri in range(2) for g8 in range(8)]
            for jj in range(J):
                if not pre:
                    S.add("act", ACT(Xbf[:, :, :, jj], X[:], AF.Copy), reads=["X"], writes=["Xbf"])
                S.add("dve", TT(P1[:], X[:], A8r, ALU.mult), reads=["X", "A8"], writes=["P1"])
                S.add("dve", TT(P2[:], X[:], A8i, ALU.mult), reads=["X", "A8"], writes=["P2"])
                S.add("dve", TT(Xn[:, 0, :], P1[:, 0, :], P2[:, 1, :], ALU.subtract), reads=["P1", "P2"], writes=["Xn0"])
                S.add("dve", TT(Xn[:, 1, :], P1[:, 1, :], P2[:, 0, :], ALU.add), reads=["P1", "P2"], writes=["Xn1"])
                S.add("dve", TT(X[:], Xn[:], W[:, :, :, jj], ALU.add), reads=["Xn0", "Xn1"] + WK, writes=["X"])
            if pre:
                S.barrier()
                continue
            for g8 in range(8):
                bank = 4 + g8 % 2
                for q in range(8):
                    g = g8 * 8 + q
                    o = self.ps[bank][:, q * J:(q + 1) * J]
                    S.add("pe", MM(o, M_T[:, g, :], U[:, g, :], True, False),
                          reads=["M_T", ("U", g8)], writes=[("ps", bank)])
                    S.add("pe", MM(o, Qb[:, g, 0, :], Xbf[:, 0, g, :], False, False),
                          reads=["Qb", "Xbf"], writes=[("ps", bank)])
                    S.add("pe", MM(o, Qb[:, g, 1, :], Xbf[:, 1, g, :], False, True),
                          reads=["Qb", "Xbf"], writes=[("ps", bank)])
                e = nev % 2
                nev += 1
                xg, g2 = ytmp[e], gtmp[e]
                S.add("act", ACT(xg[:], self.ps[bank][:], AF.Copy), reads=[("ps", bank)], writes=[("xg", e)])
                S.add("dve", TT(g2[:], xg[:], xg[:], ALU.mult), reads=[("xg", e)], writes=[("g2", e)])
                S.add("dve", TS(g2[:], g2[:], 0.044715, 1.0, ALU.mult, ALU.add), reads=[("g2", e)], writes=[("g2", e)])
                S.add("dve", TT(g2[:], g2[:], xg[:], ALU.mult), reads=[("g2", e), ("xg", e)], writes=[("g2", e)])
                S.add("act", ACT(g2[:], g2[:], AF.Sigmoid, scale=1.5957691216057308),
                      reads=[("g2", e)], writes=[("g2", e)])
                S.add("dve", TT(Yg[:, g8 * 8:(g8 + 1) * 8, :].rearrange("p a b -> p (a b)"), xg[:], g2[:], ALU.mult),
                      reads=[("xg", e), ("g2", e)], writes=[("Yg", g8)])
            for ft in range(8):
                bank = 6 + ft % 2
                for t_ in range(8):
                    for gl in range(8):
                        S.add("pe", MM(self.ps[bank][:, t_:NB:8], E[:, t_ * 8 + gl, :], Yg[:, ft * 8 + gl, :],
                                       gl == 0, gl == 7),
                              reads=["E", ("Yg", ft)], writes=[("ps", bank)])
                S.add("act" if ft % 2 == 0 else "dve",
                      (ACT(yT[:, ft, :], self.ps[bank][:], AF.Copy) if ft % 2 == 0 else CP(yT[:, ft, :], self.ps[bank][:])),
                      reads=[("ps", bank)], writes=[("yT", ft)])
            S.barrier()
            src = self.res_src()
            for sl_ in range(4):
                b = nslab % 2
                nslab += 1
                S.add("pool", DMA(wsl[b][:, 0:8, :], w_out[j, :, sl_ * 512:(sl_ + 1) * 512]
                                  .rearrange("(kt p) m -> p kt m", p=128)), writes=[("wsl", b)], dma=True)
                S.add("pool", DMA(wsl[b][:, 8:16, :], w_out[j, :, D + sl_ * 512:D + (sl_ + 1) * 512]
                                  .rearrange("(kt p) m -> p kt m", p=128)), writes=[("wslg", b)], dma=True)
                hb_ = self.hs_n % 2
                self.hs_n += 1
                S.add("sp", DMA(hs[hb_][:], src[:, sl_ * 4:sl_ * 4 + 4, t0:t0 + NB]), writes=[("hs", hb_)], dma=True)
                for f4 in range(4):
                    bv, bg = 4 + f4 % 2, 6 + f4 % 2
                    for kt in range(8):
                        S.add("pe", MM(self.ps[bv][:], wsl[b][:, kt, f4 * 128:(f4 + 1) * 128], yT[:, kt, :],
                                       kt == 0, kt == 7),
                              reads=[("wsl", b), ("yT", kt)], writes=[("ps", bv)])
                    for kt in range(8):
                        S.add("pe", MM(self.ps[bg][:], wsl[b][:, 8 + kt, f4 * 128:(f4 + 1) * 128], yT[:, kt, :],
                                       kt == 0, kt == 7),
                              reads=[("wslg", b), ("yT", kt)], writes=[("ps", bg)])
                    sg = sigt[f4 % 2]
                    sk = ("sig", f4 % 2)
                    S.add("act", ACT(sg[:], self.ps[bg][:], AF.Sigmoid), reads=[("ps", bg)], writes=[sk])
                    S.add("dve", TT(sg[:], self.ps[bv][:], sg[:], ALU.mult), reads=[("ps", bv), sk], writes=[sk])
                    S.add("dve", TT(hs[hb_][:, f4, :], hs[hb_][:, f4, :], sg[:], ALU.add),
                          reads=[sk, ("hs", hb_)], writes=[("hs", hb_)])
                S.add("sp", DMA(self.hres[:, sl_ * 4:sl_ * 4 + 4, t0:t0 + NB], hs[hb_][:]),
                      reads=[("hs", hb_)], dma=True)
            S.barrier()
        self.first_res = False

    def final_phase(self, do_norm):
        S, A = self.S, self.A
        self.phase_begin()
        NB = 512
        hb = A.alloc([128, NDT, NB], F32, "f_h")
        ob = A.alloc([128, NDT, NB], F32, "f_o")
        for blk in range(self.Lc // NB):
            t0 = blk * NB
            self.load_h(hb, t0, NB, "h")
            if do_norm:
                self.rmsnorm(hb, ob, NB, 8, "h", "o")
                src, key = ob, "o"
            else:
                src, key = hb, "h"
            for q in range(4):
                S.add("sp", DMA(self.outT[:, 4 * q:4 * q + 4, t0:t0 + NB], src[:, 4 * q:4 * q + 4, :]),
                      reads=[(key, d_, 0) for d_ in range(4 * q, 4 * q + 4)], dma=True)


def make_consts():
    c = np.zeros((128, 5, 128), np.float32)
    c[:, 0, :] = np.eye(128, dtype=np.float32)
    s = np.arange(128)[:, None]
    t = np.arange(128)[None, :]
    c[:, 1, :] = np.where((s // 64 == t // 64) & (s > t), -1.0 / 16.0, 0.0)
    c[:, 2, :] = 1.0
    c[:, 3, :] = np.where((s // 16) <= (t // 16), 1.0, 0.0)
    c[:, 4, 0] = np.where(np.arange(128) < 64, -1.0 / 16.0, 0.0)
    c[:, 4, 1] = np.where(np.arange(128) >= 64, -1.0 / 16.0, 0.0)
    return c


def make_E():
    E = np.zeros((128, 64, 128), np.float32)
    for a in range(8):
        for b in range(8):
            for c in range(16):
                E[a * 16 + c, a * 8 + b, b * 16 + c] = 1.0
    return E


def col_layout(v):
    v = np.asarray(v, np.float32)
    lead = v.shape[:-1]
    r = v.reshape(*lead, NDT, 128)
    r = np.moveaxis(r, -1, 0)
    return np.ascontiguousarray(r)


def prep_shared(inp):
    f = lambda k: np.asarray(inp[k], np.float32)
    gains = np.zeros((128, 9, NDT), np.float32)
    gains[:, 0:2] = col_layout(f("gla_norm"))
    gains[:, 2:4] = col_layout(f("s5_norm"))
    gains[:, 4:8] = col_layout(f("mlp_norm"))
    gains[:, 8] = col_layout(f("final_norm"))
    wg = np.concatenate([f("gla_w_gate_up"), f("gla_b_gate")[:, None, :]], axis=1)
    on = np.ascontiguousarray(f("gla_o_norm").reshape(2, 4, 128).transpose(0, 2, 1))
    lam = np.stack([f("s5_lam_re").transpose(0, 2, 1), f("s5_lam_im").transpose(0, 2, 1),
                    np.broadcast_to(f("s5_log_dt")[:, None, :], (2, 64, 64))], axis=2)
    Bm = np.stack([f("s5_b_re").transpose(0, 2, 1, 3), f("s5_b_im").transpose(0, 2, 1, 3)], axis=2)
    Cm = np.stack([f("s5_c_re").transpose(0, 3, 1, 2), f("s5_c_im").transpose(0, 3, 1, 2)], axis=2)
    dsk = f("s5_d").reshape(2, 64, 16)
    dcol = np.broadcast_to(dsk.transpose(0, 2, 1)[:, None, :, :], (2, 8, 16, 64)).reshape(2, 128, 64)
    return {
        "gains": gains, "consts": make_consts(),
        "gla_w_in": f("gla_w_in"), "gla_wg": np.ascontiguousarray(wg), "gla_on": on,
        "gla_w_out": f("gla_w_out"),
        "s5_w_in": f("s5_w_in"), "s5_w_out": f("s5_w_out"),
        "s5_lam": np.ascontiguousarray(lam), "s5_B": np.ascontiguousarray(Bm),
        "s5_C": np.ascontiguousarray(Cm), "s5_dcol": np.ascontiguousarray(dcol),
        "s5_E": make_E(),
        "mlp_w_up": f("mlp_w_up"), "mlp_w_down": f("mlp_w_down"),
    }


def x_to_T(xs):
    L = xs.shape[0]
    return np.ascontiguousarray(xs.reshape(L, NDT, 128).transpose(2, 1, 0))


def T_to_x(oT):
    L = oT.shape[2]
    return np.ascontiguousarray(oT.transpose(2, 1, 0).reshape(L, D))


FULL_PHASES = []
for _i in range(DEPTH):
    FULL_PHASES.append(("gla" if _i % 2 == 0 else "s5", _i // 2))
    FULL_PHASES.append(("mlp", _i))

_CACHE = {}


def kernel(**inputs):
    x = np.asarray(inputs["x"], np.float32)
    shared = prep_shared(inputs)
    ncores = 8
    Lc = SEQ // 2
    key = (Lc, "full")
    if key not in _CACHE:
        _CACHE[key] = Builder(Lc, FULL_PHASES).nc
    nc = _CACHE[key]
    in_maps = []
    for c in range(ncores):
        b, half = c // 2, c % 2
        m = dict(shared)
        m["xT"] = x_to_T(x[b, half * Lc:(half + 1) * Lc])
        m["flag"] = np.full((128, 1), float(half), np.float32)
        in_maps.append(m)
    res = run_bass_kernel_spmd(nc, in_maps, core_ids=list(range(ncores)))
    out = np.zeros((BATCH, SEQ, D), np.float32)
    for c in range(ncores):
        b, half = c // 2, c % 2
        out[b, half * Lc:(half + 1) * Lc] = T_to_x(res.results[c]["outT"])
    return out
```
